# Optimizing a Trainium2 kernel written in Bass

```python
import math
import jax, jax.numpy as jnp
from jax import lax
import numpy as np

D_MODEL = 4096
BATCH = 1
SEQ = 8192
DEPTH = 2

HEAD_DIM_A = 128
N_HEADS_A = (D_MODEL // 2) // HEAD_DIM_A
DILATED_BRANCHES = ((128, 1), (512, 4), (2048, 16))
HEAD_DIM_B = 64
N_Q_HEADS_B = (D_MODEL // 2) // HEAD_DIM_B
N_KV_HEADS_B = N_Q_HEADS_B // 8
GQA_GROUP = N_Q_HEADS_B // N_KV_HEADS_B
WINDOW_B = 128
WIDTH_A = N_HEADS_A * HEAD_DIM_A
WIDTH_B = N_Q_HEADS_B * HEAD_DIM_B
MIX_WIDTH = WIDTH_A + WIDTH_B
KV_WIDTH_B = N_KV_HEADS_B * HEAD_DIM_B
IN_WIDTH = 3 * WIDTH_A + WIDTH_B + 2 * KV_WIDTH_B
N_ALIBI_HEADS = N_HEADS_A + N_Q_HEADS_B
D_FF = 256 * math.ceil(8 * D_MODEL / 3 / 256)
CONV_WIDTH = 3
BLOCK = 128
N_MOD = 6
DEEPNORM_ALPHA = (2 * DEPTH) ** 0.25
DEEPNORM_BETA = (8 * DEPTH) ** -0.25
LN_EPS = 1e-5

kernel_name = "hybrid_dilated_swa_sink_convffn_deepnorm"


def alibi_slopes():
    i = jnp.arange(1, N_ALIBI_HEADS + 1, dtype=jnp.float32)
    return jnp.exp2(-8.0 * i / N_ALIBI_HEADS)


def layer_norm(x, g, b):
    xf = x.astype(jnp.float32)
    mu = xf.mean(-1, keepdims=True)
    var = jnp.square(xf - mu).mean(-1, keepdims=True)
    y = (xf - mu) * lax.rsqrt(var + LN_EPS) * g.astype(jnp.float32) + b.astype(jnp.float32)
    return y.astype(x.dtype)


def banded_attention(q, k, v, slopes, max_dist, dist_scale, sinks=None):
    n, L, hk, g, dh = q.shape
    nb = -(-L // BLOCK)
    pad = nb * BLOCK - L
    q = jnp.pad(q, ((0, 0), (0, pad), (0, 0), (0, 0), (0, 0)))
    k = jnp.pad(k, ((0, 0), (0, pad), (0, 0), (0, 0)))
    v = jnp.pad(v, ((0, 0), (0, pad), (0, 0), (0, 0)))
    qb = q.reshape(n, nb, BLOCK, hk, g, dh)

    def with_prev(t):
        tb = t.reshape(n, nb, BLOCK, hk, dh)
        prev = jnp.pad(tb[:, :-1], ((0, 0), (1, 0), (0, 0), (0, 0), (0, 0)))
        return jnp.concatenate([prev, tb], axis=2)

    kk, vv = with_prev(k), with_prev(v)
    s = jnp.einsum('nbqhgd,nbkhd->nbhgqk', qb, kk,
                   preferred_element_type=jnp.float32) * (dh ** -0.5)
    qpos = jnp.arange(BLOCK)[:, None] + BLOCK
    kpos = jnp.arange(2 * BLOCK)[None, :]
    dist = qpos - kpos
    band = (dist >= 0) & (dist <= max_dist)
    has_prev = (jnp.arange(nb)[:, None, None] > 0) | (kpos >= BLOCK)[None]
    valid = band[None] & has_prev
    bias = -(slopes.astype(jnp.float32) * dist_scale)[:, :, None, None] * dist.astype(jnp.float32)
    s = jnp.where(valid[:, None, None], s + bias, -jnp.inf)
    lse = jax.nn.logsumexp(s, axis=-1)
    if sinks is not None:
        lse = jnp.logaddexp(lse, sinks.astype(jnp.float32)[..., None])
    p = jnp.exp(s - lse[..., None]).astype(v.dtype)
    o = jnp.einsum('nbhgqk,nbkhd->nbqhgd', p, vv).reshape(n, nb * BLOCK, hk, g, dh)[:, :L]
    lse = lse.transpose(0, 1, 4, 2, 3).reshape(n, nb * BLOCK, hk, g)[:, :L]
    return o, lse


def dilated_attention(q, k, v, slopes):
    b, s, h, dh = q.shape
    outs, lses = [], []
    for window, dil in DILATED_BRANCHES:
        L = s // dil

        def to_strided(t):
            return t.reshape(b, L, dil, h, dh).transpose(0, 2, 1, 3, 4).reshape(b * dil, L, h, dh)

        o, lse = banded_attention(to_strided(q)[:, :, :, None], to_strided(k), to_strided(v),
                                  slopes[:, None], window // dil, dil)
        o = o[:, :, :, 0].reshape(b, dil, L, h, dh).transpose(0, 2, 1, 3, 4).reshape(b, s, h, dh)
        lse = lse[..., 0].reshape(b, dil, L, h).transpose(0, 2, 1, 3).reshape(b, s, h)
        outs.append(o)
        lses.append(lse)
    w = jax.nn.softmax(jnp.stack(lses, 0), axis=0)
    return jnp.einsum('rbsh,rbshd->bshd', w.astype(q.dtype), jnp.stack(outs, 0))


def causal_depthwise_conv(h, w, b):
    out = lax.conv_general_dilated(h, w[:, None, :].astype(h.dtype), window_strides=(1,),
                                   padding=[(CONV_WIDTH - 1, 0)],
                                   dimension_numbers=('NWC', 'WIO', 'NWC'),
                                   feature_group_count=h.shape[-1])
    return out + b


def setup_inputs(seed: int = 0) -> dict:
    key = jax.random.key(seed)
    ks = jax.random.split(key, 16)
    f32 = jnp.float32
    nrm = lambda k, shape, sc: jax.random.normal(k, shape, f32) * sc
    beta = DEEPNORM_BETA
    col_scale = np.concatenate([
        np.ones(2 * WIDTH_A), np.full(WIDTH_A, beta),
        np.ones(WIDTH_B + KV_WIDTH_B), np.full(KV_WIDTH_B, beta)
    ]).astype(np.float32)
    return {
        "x": nrm(ks[0], (BATCH, SEQ, D_MODEL), 1.0),
        "c": nrm(ks[1], (BATCH, D_MODEL), 1.0),
        "w_mod": nrm(ks[2], (DEPTH, D_MODEL, N_MOD * D_MODEL), 0.2 * D_MODEL ** -0.5),
        "b_mod": nrm(ks[3], (DEPTH, N_MOD * D_MODEL), 0.02),
        "w_in": nrm(ks[4], (DEPTH, D_MODEL, IN_WIDTH), D_MODEL ** -0.5) * jnp.asarray(col_scale),
        "sinks": nrm(ks[5], (DEPTH, N_Q_HEADS_B), 1.0),
        "w_out": nrm(ks[6], (DEPTH, MIX_WIDTH, D_MODEL), beta * MIX_WIDTH ** -0.5),
        "ln1_g": 1.0 + nrm(ks[7], (DEPTH, D_MODEL), 0.02),
        "ln1_b": nrm(ks[8], (DEPTH, D_MODEL), 0.02),
        "w_up": nrm(ks[9], (DEPTH, D_MODEL, 2 * D_FF), D_MODEL ** -0.5),
        "conv_w": nrm(ks[10], (DEPTH, CONV_WIDTH, 2 * D_FF), CONV_WIDTH ** -0.5),
        "conv_b": nrm(ks[11], (DEPTH, 2 * D_FF), 0.02),
        "w_down": nrm(ks[12], (DEPTH, D_FF, D_MODEL), beta * D_FF ** -0.5),
        "ln2_g": 1.0 + nrm(ks[13], (DEPTH, D_MODEL), 0.02),
        "ln2_b": nrm(ks[14], (DEPTH, D_MODEL), 0.02),
    }


def reference(x, c, w_mod, b_mod, w_in, sinks, w_out, ln1_g, ln1_b,
              w_up, conv_w, conv_b, w_down, ln2_g, ln2_b):
    b, s, _ = x.shape
    slopes = alibi_slopes()
    slopes_b = slopes[:N_Q_HEADS_B].reshape(N_KV_HEADS_B, GQA_GROUP)
    slopes_a = slopes[N_Q_HEADS_B:]
    split_at = [WIDTH_A, 2 * WIDTH_A, 3 * WIDTH_A, 3 * WIDTH_A + WIDTH_B,
                3 * WIDTH_A + WIDTH_B + KV_WIDTH_B]
    for l in range(DEPTH):
        mod = (jax.nn.silu(c) @ w_mod[l] + b_mod[l])[:, None, :]
        shift_a, scale_a, gate_a, shift_m, scale_m, gate_m = jnp.split(mod, N_MOD, axis=-1)

        u = x * (1.0 + scale_a) + shift_a
        h = u @ w_in[l]
        qa, ka, va, qb, kb, vb = jnp.split(h, split_at, axis=-1)
        hd_a = (b, s, N_HEADS_A, HEAD_DIM_A)
        oa = dilated_attention(qa.reshape(hd_a), ka.reshape(hd_a), va.reshape(hd_a), slopes_a)
        ob, _ = banded_attention(qb.reshape(b, s, N_KV_HEADS_B, GQA_GROUP, HEAD_DIM_B),
                                 kb.reshape(b, s, N_KV_HEADS_B, HEAD_DIM_B),
                                 vb.reshape(b, s, N_KV_HEADS_B, HEAD_DIM_B),
                                 slopes_b, WINDOW_B - 1, 1,
                                 sinks[l].reshape(N_KV_HEADS_B, GQA_GROUP))
        mixed = jnp.concatenate([oa.reshape(b, s, WIDTH_A), ob.reshape(b, s, WIDTH_B)], axis=-1)
        att = mixed @ w_out[l]
        x = layer_norm(DEEPNORM_ALPHA * x + (1.0 + gate_a) * att, ln1_g[l], ln1_b[l])

        u = x * (1.0 + scale_m) + shift_m
        hup = causal_depthwise_conv(u @ w_up[l], conv_w[l], conv_b[l])
        g, val = jnp.split(hup, 2, axis=-1)
        y = (jax.nn.silu(g) * val) @ w_down[l]
        x = layer_norm(DEEPNORM_ALPHA * x + (1.0 + gate_m) * y, ln2_g[l], ln2_b[l])
    return x
```

```python
import math
import numpy as np
import concourse.bass as bass
import concourse.mybir as mybir
from concourse.bass_utils import run_bass_kernel_spmd

F32 = mybir.dt.float32
BF16 = mybir.dt.bfloat16
AF = mybir.ActivationFunctionType
ALU = mybir.AluOpType
NEG = -30000.0
LN_EPS = 1e-5


class Cfg:
    def __init__(self, S=8192, D=4096, DEPTH=2):
        self.S, self.D, self.DEPTH = S, D, DEPTH
        self.KC = D // 128
        self.HA = (D // 2) // 128
        self.HB = (D // 2) // 64
        self.HKV = self.HB // 8
        self.WA = self.HA * 128
        self.WB = self.HB * 64
        self.KVB = self.HKV * 64
        self.INW = 3 * self.WA + self.WB + 2 * self.KVB
        self.DFF = 256 * math.ceil(8 * D / 3 / 256)
        self.FC = self.DFF // 128
        self.QKW = 2 * self.WA + self.WB + self.KVB
        self.VW = self.WA + self.KVB
        self.alpha = (2 * DEPTH) ** 0.25
        n_al = self.HA + self.HB
        i = np.arange(1, n_al + 1, dtype=np.float32)
        self.slopes = np.exp2(np.float32(-8.0) * i / np.float32(n_al)).astype(np.float32)


class Buf:
    def __init__(self, ap, name=""):
        self.ap = ap
        self.name = name
        self.subs = {}

    def sub(self, i):
        if i not in self.subs:
            self.subs[i] = ("sub", id(self), i)
        return (self, i)


class Sched:
    ENG = ["pe", "act", "dve", "pool", "sp"]

    def __init__(self, nc, ndma=6):
        self.nc = nc
        self.ops = {e: [] for e in self.ENG}
        self.needed = {e: set() for e in self.ENG}
        self.csem = {e: nc.alloc_semaphore(name=f"c_{e}") for e in self.ENG}
        self.dsem = {e: [nc.alloc_semaphore(name=f"d_{e}{i}") for i in range(ndma)] for e in ("sp", "pool", "act")}
        self.dcnt = {(e, i): 0 for e in ("sp", "pool", "act") for i in range(ndma)}
        self.drr = {e: 0 for e in ("sp", "pool", "act")}
        self.ndma = ndma
        self.last_w = {}
        self.readers = {}
        self.fence_tokens = []

    def _keys(self, k):
        if isinstance(k, tuple):
            b, i = k
            return [b.subs[i]], [("whole", id(b))]
        ks = [("whole", id(k))] + list(k.subs.values())
        return ks, []

    @staticmethod
    def _stream(tok):
        return (tok[0], tok[1]) if tok[0] == "c" else (tok[0], tok[1], tok[2])

    def _mark(self, tok):
        if tok[0] == "c":
            self.needed[tok[1]].add(tok[2])

    def op(self, eng, fn, reads=(), writes=(), dma=False):
        deps = {}

        def add(tok):
            if tok[0] == "c" and tok[1] == "pe" and eng == "pe" and not dma:
                return
            s = self._stream(tok)
            if s not in deps or deps[s][-1] < tok[-1]:
                deps[s] = tok

        for t in self.fence_tokens:
            add(t)
        for k in reads:
            own, chk = self._keys(k)
            for kk in own + chk:
                if kk in self.last_w:
                    add(self.last_w[kk])
        for k in writes:
            own, chk = self._keys(k)
            for kk in own + chk:
                if kk in self.last_w:
                    add(self.last_w[kk])
                for t in self.readers.get(kk, {}).values():
                    add(t)
        if dma:
            slot = self.drr[eng]
            self.drr[eng] = (slot + 1) % self.ndma
            prev = self.dcnt[(eng, slot)]
            if prev > 0:
                add(("d", eng, slot, prev))
            val = prev + 16
            self.dcnt[(eng, slot)] = val
            tok = ("d", eng, slot, val)
        else:
            tok = ("c", eng, len(self.ops[eng]))
        dl = list(deps.values())
        for t in dl:
            self._mark(t)
        self.ops[eng].append((fn, dl, tok))
        for k in writes:
            own, _ = self._keys(k)
            for kk in own:
                self.last_w[kk] = tok
                self.readers[kk] = {}
        for k in reads:
            own, _ = self._keys(k)
            for kk in own:
                self.readers.setdefault(kk, {})[self._stream(tok)] = tok
        return tok

    def fence(self):
        toks = []
        for e in self.ENG:
            for idx in range(len(self.ops[e]) - 1, -1, -1):
                if self.ops[e][idx][2][0] == "c":
                    t = ("c", e, idx)
                    toks.append(t)
                    self._mark(t)
                    break
        for (e, i), v in self.dcnt.items():
            if v > 0:
                toks.append(("d", e, i, v))
        self.fence_tokens = toks
        self.last_w = {}
        self.readers = {}

    def emit(self, block):
        cval = {}
        for e in self.ENG:
            m = {}
            c = 0
            for idx, (fn, dl, tok) in enumerate(self.ops[e]):
                if tok[0] == "c" and idx in self.needed[e]:
                    c += 1
                    m[idx] = c
            cval[e] = m
        final_dma = dict(self.dcnt)
        sched = self

        def run(e, E):
            waited = {}

            def wait(sem, key, val):
                if waited.get(key, 0) < val:
                    E.wait_ge(sem, val)
                    waited[key] = val

            for idx, (fn, dl, tok) in enumerate(sched.ops[e]):
                for d in dl:
                    if d[0] == "c":
                        wait(sched.csem[d[1]], ("c", d[1]), cval[d[1]][d[2]])
                    else:
                        wait(sched.dsem[d[1]][d[2]], ("d", d[1], d[2]), d[3])
                ins = fn(E)
                if tok[0] == "d":
                    ins.then_inc(sched.dsem[tok[1]][tok[2]], 16)
                elif idx in sched.needed[e]:
                    ins.then_inc(sched.csem[e], 1)
            if e == "sp":
                for (de, i), v in final_dma.items():
                    if v > 0:
                        wait(sched.dsem[de][i], ("d", de, i), v)

        @block.sync
        def _(E):
            run("sp", E)

        @block.tensor
        def _(E):
            run("pe", E)

        @block.scalar
        def _(E):
            run("act", E)

        @block.vector
        def _(E):
            run("dve", E)

        @block.gpsimd
        def _(E):
            run("pool", E)


class Arena:
    def __init__(self, nc, nbytes):
        self.t = nc.alloc_sbuf_tensor("arena", [128, nbytes // 4], F32)
        self.nbytes = nbytes
        self.off = 0

    def reset(self):
        self.off = 0

    def alloc(self, free_shape, dtype, name=""):
        esz = 2 if dtype == BF16 else 4
        n = int(np.prod(free_shape))
        nb = (n * esz + 31) // 32 * 32
        assert self.off + nb <= self.nbytes, f"arena overflow {name} {self.off}+{nb}>{self.nbytes}"
        w0 = self.off // 4
        ap = self.t[:, w0:w0 + nb // 4]
        self.off += nb
        if dtype != F32:
            ap = ap.bitcast(dtype)
        ap = ap[:, 0:n]
        if len(free_shape) == 2:
            ap = ap.rearrange("p (a b) -> p a b", a=free_shape[0], b=free_shape[1])
        elif len(free_shape) == 3:
            ap = ap.rearrange("p (a b c) -> p a b c", a=free_shape[0], b=free_shape[1], c=free_shape[2])
        return Buf(ap, name)


def build_program(cfg, plan, debug=False):
    S, D, KC = cfg.S, cfg.D, cfg.KC
    DEPTH = plan.get("depth", 1)
    HA, HB, HKV, WA, WB, KVB = cfg.HA, cfg.HB, cfg.HKV, cfg.WA, cfg.WB, cfg.KVB
    DFF, FC, QKW, VW = cfg.DFF, cfg.FC, cfg.QKW, cfg.VW
    NT = S // 128
    NB512 = S // 512
    nc = bass.Bass("TRN2", target_bir_lowering=False)
    used_inputs = []

    in_specs = {
        "x": [S, D], "c": [KC, 128], "w_mod": [DEPTH * D, 6 * D], "b_mod": [DEPTH, 6 * D],
        "w_in": [DEPTH * D, cfg.INW], "sinks": [DEPTH * HB // 2, 128], "w_out": [DEPTH * D, D],
        "ln1_g": [DEPTH, D], "ln1_b": [DEPTH, D], "w_up": [DEPTH * D, 2 * DFF],
        "conv_w": [DEPTH * 3, 2 * DFF], "conv_b": [DEPTH, 2 * DFF], "w_down": [DEPTH * DFF, D],
        "ln2_g": [DEPTH, D], "ln2_b": [DEPTH, D], "consts": [128, 128 + 3 * 256],
    }
    sc_specs = {
        "mod_d": ([DEPTH, 6 * D], F32), "uT_d": ([NB512 * 128, KC * 512], BF16), "qkT_d": ([QKW, S], BF16), "v_d": ([S, VW], BF16),
        "mixT_d": ([NB512 * 128, KC * 512], BF16), "y_d": ([S, D], F32), "x1_d": ([S, D], F32), "x2_d": ([S, D], F32),
        "aT_d": ([NT * 128, FC * 128], BF16),
    }
    cache = {}

    class _T:
        def __getattr__(self, name):
            if name in cache:
                return cache[name]
            if name in in_specs:
                ap = nc.dram_tensor(name, list(in_specs[name]), F32, kind="ExternalInput").ap()
                used_inputs.append(name)
            else:
                shape, dt = sc_specs[name]
                if name in plan.get("ext_in", ()):
                    ap = nc.dram_tensor(name, list(shape), dt, kind="ExternalInput").ap()
                    used_inputs.append(name)
                elif name in plan.get("ext_out", ()) or debug:
                    ap = nc.dram_tensor(name, list(shape), dt, kind="ExternalOutput").ap()
                else:
                    ap = nc.dram_tensor(name, list(shape), dt).ap()
            cache[name] = ap
            return ap

    T = _T()

    def blk(act_d, KCn, tb):
        return act_d[tb * 128:(tb + 1) * 128, :].rearrange("p (k s) -> p k s", k=KCn)

    sch = Sched(nc)
    arena = Arena(nc, 176 * 1024)
    cst_t = nc.alloc_sbuf_tensor("cst", [128, 128 + 3 * 256], F32)
    cstb_t = nc.alloc_sbuf_tensor("cstb", [128, 3 * 128], BF16)
    CST = Buf(cst_t[:], "cst")
    CSTB = Buf(cstb_t[:], "cstb")
    ident = cst_t[:, 0:128]
    DD = cst_t[:, 128:384]
    MA = cst_t[:, 384:640]
    MB = cst_t[:, 640:896]
    ones_b = cstb_t[:, 0:128]
    onesz = [cstb_t[:, 128:256], cstb_t[:, 256:384]]

    psum_ctx = [nc.psum_tensor(f"ps{i}", [128, 512], F32) for i in range(8)]
    ps_t = [c.__enter__() for c in psum_ctx]
    PS = [Buf(t[:], f"ps{i}") for i, t in enumerate(ps_t)]

    sch.op("sp", lambda E: E.dma_start(out=cst_t[:], in_=T.consts), writes=[CST], dma=True)
    sch.op("pool", lambda E: E.memset(cstb_t[:, 0:128], 1.0), writes=[CSTB])
    sch.op("pool", lambda E: E.memset(cstb_t[:, 128:384], 0.0), writes=[CSTB])
    sch.op("pool", lambda E: E.memset(cstb_t[:, 128:192], 1.0), writes=[CSTB])
    sch.op("pool", lambda E: E.memset(cstb_t[:, 320:384], 1.0), writes=[CSTB])

    def new_phase():
        sch.fence()
        arena.reset()

    def load_pp(src_row_ap, dst, tmp, psb, add_one=False):
        sch.op("sp", lambda E: E.dma_start(out=tmp.ap[0:KC, 0:128], in_=src_row_ap.rearrange("(k p) -> k p", p=128)),
               writes=[tmp], dma=True)
        sch.op("pe", lambda E: E.transpose(out=psb.ap[:, 0:KC], in_=tmp.ap[0:KC, 0:128], identity=ident[0:KC, 0:KC]),
               reads=[tmp, CST], writes=[psb])
        if add_one:
            sch.op("dve", lambda E: E.tensor_scalar(out=dst.ap, in0=psb.ap[:, 0:KC], scalar1=1.0, scalar2=None, op0=ALU.add),
                   reads=[psb], writes=[dst])
        else:
            sch.op("dve", lambda E: E.tensor_copy(out=dst.ap, in_=psb.ap[:, 0:KC]), reads=[psb], writes=[dst])

    def phase_mod():
        new_phase()
        crow = arena.alloc([128], F32, "crow")
        srow = arena.alloc([128], F32, "srow")
        sc = arena.alloc([KC], F32, "sc")
        sch.op("sp", lambda E: E.dma_start(out=crow.ap[0:KC, :], in_=T.c), writes=[crow], dma=True)
        sch.op("act", lambda E: E.activation(out=srow.ap[0:KC, :], in_=crow.ap[0:KC, :], func=AF.Silu),
               reads=[crow], writes=[srow])
        sch.op("pe", lambda E: E.transpose(out=PS[0].ap[:, 0:KC], in_=srow.ap[0:KC, :], identity=ident[0:KC, 0:KC]),
               reads=[srow, CST], writes=[PS[0]])
        sch.op("dve", lambda E: E.tensor_copy(out=sc.ap, in_=PS[0].ap[:, 0:KC]), reads=[PS[0]], writes=[sc])
        wb = [arena.alloc([KC, 512], F32, f"wmod{i}") for i in range(2)]
        bb = [arena.alloc([512], F32, f"bm{i}") for i in range(2)]
        rb = [arena.alloc([512], F32, f"rm{i}") for i in range(2)]
        it = 0
        for l in range(DEPTH):
            wv = T.w_mod[l * D:(l + 1) * D, :].rearrange("(k p) c -> p k c", p=128)
            for n in range(6 * D // 512):
                W = wb[it % 2]; Bm = bb[it % 2]; R = rb[it % 2]; P = PS[1 + it % 2]
                sch.op("sp", lambda E, W=W, n=n, wv=wv: E.dma_start(out=W.ap, in_=wv[:, :, n * 512:(n + 1) * 512]),
                       writes=[W], dma=True)
                sch.op("sp", lambda E, Bm=Bm, n=n, l=l: E.dma_start(out=Bm.ap[0:1, :], in_=T.b_mod[l:l + 1, n * 512:(n + 1) * 512]),
                       writes=[Bm], dma=True)
                for k in range(KC):
                    sch.op("pe", lambda E, W=W, P=P, k=k: E.matmul(P.ap[0:1, :], lhsT=sc.ap[:, k:k + 1], rhs=W.ap[:, k, :],
                                                                  start=(k == 0), stop=(k == KC - 1)),
                           reads=[W, sc], writes=[P])
                sch.op("dve", lambda E, P=P, Bm=Bm, R=R: E.tensor_tensor(out=R.ap[0:1, :], in0=P.ap[0:1, :], in1=Bm.ap[0:1, :], op=ALU.add),
                       reads=[P, Bm], writes=[R])
                sch.op("sp", lambda E, R=R, n=n, l=l: E.dma_start(out=T.mod_d[l:l + 1, n * 512:(n + 1) * 512], in_=R.ap[0:1, :]),
                       reads=[R], dma=True)
                it += 1

    def phase_u(l, src_d, m_shift, m_scale):
        new_phase()
        tmp = arena.alloc([128], F32, "pp_tmp")
        tmp2 = arena.alloc([128], F32, "pp_tmp2")
        sc1p = arena.alloc([KC], F32, "sc1p")
        sh = arena.alloc([KC], F32, "sh")
        load_pp(T.mod_d[l, m_scale * D:(m_scale + 1) * D], sc1p, tmp, PS[0], add_one=True)
        load_pp(T.mod_d[l, m_shift * D:(m_shift + 1) * D], sh, tmp2, PS[1], add_one=False)
        xb = [arena.alloc([D], F32, f"xb{i}") for i in range(2)]
        ub = [arena.alloc([KC, 512], BF16, f"ub{i}") for i in range(2)]
        pi = 0
        for tt in range(NT):
            X = xb[tt % 2]
            U = ub[(tt // 4) % 2]
            j = tt % 4
            sch.op("sp", lambda E, X=X, tt=tt: E.dma_start(out=X.ap, in_=src_d[tt * 128:(tt + 1) * 128, :]), writes=[X], dma=True)
            for k4 in range(KC // 4):
                P = PS[2 + pi % 6]; pi += 1
                for kk in range(4):
                    k = k4 * 4 + kk
                    sch.op("pe", lambda E, P=P, X=X, k=k, kk=kk: E.transpose(out=P.ap[:, kk * 128:(kk + 1) * 128],
                                                                            in_=X.ap[:, k * 128:(k + 1) * 128], identity=ident),
                           reads=[X, CST], writes=[P])
                for kk in range(4):
                    k = k4 * 4 + kk
                    sch.op("act", lambda E, P=P, U=U, k=k, kk=kk, j=j: E.activation(
                        out=U.ap[:, k, j * 128:(j + 1) * 128], in_=P.ap[:, kk * 128:(kk + 1) * 128], func=AF.Identity,
                        scale=sc1p.ap[:, k:k + 1], bias=sh.ap[:, k:k + 1]),
                        reads=[P, sc1p, sh], writes=[U.sub((k, j))])
            if j == 3:
                tb = tt // 4
                sch.op("sp", lambda E, U=U, tb=tb: E.dma_start(out=blk(T.uT_d, KC, tb), in_=U.ap), reads=[U], dma=True)

    def gemm_F(w_rows_ap, KCn, col_tiles, actT_d, G, evac):
        wv = w_rows_ap.rearrange("(k p) c -> p k c", p=128)
        wbuf = [arena.alloc([KCn, G * 128], BF16, f"gw{i}") for i in range(2)]
        abuf = [arena.alloc([KCn, 512], BF16, f"ga{i}") for i in range(2)]
        groups = [col_tiles[i:i + G] for i in range(0, len(col_tiles), G)]
        ai = 0
        pi = 0
        for gi, grp in enumerate(groups):
            W = wbuf[gi % 2]
            runs = []
            for ti, (c0, cw) in enumerate(grp):
                if runs and runs[-1][1] + runs[-1][2] == c0 and runs[-1][2] % 128 == 0:
                    runs[-1][2] += cw; runs[-1][3].append(ti)
                else:
                    runs.append([ti, c0, cw, [ti]])
            for (ti0, c0, tw, tis) in runs:
                sch.op("pool", lambda E, W=W, ti0=ti0, c0=c0, tw=tw: E.dma_start(out=W.ap[:, :, ti0 * 128:ti0 * 128 + tw], in_=wv[:, :, c0:c0 + tw]),
                       writes=[W.sub(t_) for t_ in tis], dma=True)
            for tb in range(NB512):
                A = abuf[ai % 2]; ai += 1
                sch.op("sp", lambda E, A=A, tb=tb: E.dma_start(out=A.ap, in_=blk(actT_d, KCn, tb)), writes=[A], dma=True)
                for ti, (c0, cw) in enumerate(grp):
                    P = PS[pi % 8]; pi += 1
                    for k in range(KCn):
                        sch.op("pe", lambda E, P=P, W=W, A=A, k=k, ti=ti, cw=cw: E.matmul(
                            P.ap[0:cw, :], lhsT=W.ap[:, k, ti * 128:ti * 128 + cw], rhs=A.ap[:, k, :], start=(k == 0), stop=(k == KCn - 1)),
                            reads=[W.sub(ti), A], writes=[P])
                    evac(gi * G + ti, tb, P, cw)

    def gemm_T(w_rows_ap, KCn, col_tiles, actT_d, TB, evac):
        wv = w_rows_ap.rearrange("(k p) c -> p k c", p=128)
        W = arena.alloc([KCn, 512], BF16, "tw")
        abuf = [arena.alloc([KCn, TB], BF16, f"ta{i}") for i in range(2)]
        ai = 0
        pi = 0
        for ci, (c0, cw) in enumerate(col_tiles):
            sch.op("pool", lambda E, c0=c0, cw=cw: E.dma_start(out=W.ap[:, :, 0:cw], in_=wv[:, :, c0:c0 + cw]), writes=[W], dma=True)
            for tb in range(S // TB):
                A = abuf[ai % 2]; ai += 1
                sch.op("sp", lambda E, A=A, tb=tb: E.dma_start(out=A.ap, in_=blk(actT_d, KCn, tb)), writes=[A], dma=True)
                for t in range(TB // 128):
                    P = PS[pi % 8]; pi += 1
                    for k in range(KCn):
                        sch.op("pe", lambda E, P=P, A=A, k=k, t=t, cw=cw: E.matmul(
                            P.ap[:, 0:cw], lhsT=A.ap[:, k, t * 128:(t + 1) * 128], rhs=W.ap[:, k, 0:cw], start=(k == 0), stop=(k == KCn - 1)),
                            reads=[W, A], writes=[P])
                    evac(ci, tb * (TB // 128) + t, P, c0, cw)

    def phase_inproj(l):
        new_phase()
        wl = T.w_in[l * D:(l + 1) * D, :]
        ob = [arena.alloc([4, 512], BF16, f"qko{i}") for i in range(2)]
        tiles = []
        for c0 in range(0, 2 * WA, 128):
            tiles.append((c0, 128, c0))
        for c0 in range(3 * WA, 3 * WA + WB + KVB, 128):
            cw = min(128, 3 * WA + WB + KVB - c0)
            tiles.append((c0, cw, 2 * WA + (c0 - 3 * WA)))
        st = {"n": 0}

        def evac(ti, tb, P, cw):
            r0 = tiles[ti][2]
            O = ob[(st["n"] // 4) % 2]; slot = st["n"] % 4; st["n"] += 1
            eng = "act" if st["n"] % 2 == 0 else "dve"
            if eng == "act":
                sch.op("act", lambda E: E.copy(out=O.ap[0:cw, slot, :], in_=P.ap[0:cw, :]), reads=[P], writes=[O.sub(slot)])
            else:
                sch.op("dve", lambda E: E.tensor_copy(out=O.ap[0:cw, slot, :], in_=P.ap[0:cw, :]), reads=[P], writes=[O.sub(slot)])
            sch.op("sp", lambda E: E.dma_start(out=T.qkT_d[r0:r0 + cw, tb * 512:(tb + 1) * 512], in_=O.ap[0:cw, slot, :]),
                   reads=[O.sub(slot)], dma=True)

        gemm_F(wl, KC, [(c0, cw) for (c0, cw, _) in tiles], T.uT_d, 4, evac)

    def phase_vproj(l):
        new_phase()
        wl = T.w_in[l * D:(l + 1) * D, :]
        ob = [arena.alloc([512], BF16, f"vo{i}") for i in range(4)]
        tiles = []
        dst = []
        for c0 in range(2 * WA, 3 * WA, 512):
            cw = min(512, 3 * WA - c0)
            tiles.append((c0, cw)); dst.append(c0 - 2 * WA)
        c0 = 3 * WA + WB + KVB
        tiles.append((c0, KVB)); dst.append(WA)
        st = {"n": 0}

        def evac(ci, tt, P, c0, cw):
            O = ob[st["n"] % 4]; st["n"] += 1
            d0 = dst[ci]
            if st["n"] % 2 == 0:
                sch.op("act", lambda E: E.copy(out=O.ap[:, 0:cw], in_=P.ap[:, 0:cw]), reads=[P], writes=[O])
            else:
                sch.op("dve", lambda E: E.tensor_copy(out=O.ap[:, 0:cw], in_=P.ap[:, 0:cw]), reads=[P], writes=[O])
            sch.op("sp", lambda E: E.dma_start(out=T.v_d[tt * 128:(tt + 1) * 128, d0:d0 + cw], in_=O.ap[:, 0:cw]), reads=[O], dma=True)

        gemm_T(wl, KC, tiles, T.uT_d, 512, evac)

    def phase_attn(l):
        new_phase()
        mixv = T.mixT_d.rearrange("(t p) (k s) -> p t k s", p=128, k=KC)
        q_sb = arena.alloc([S], BF16, "q_sb")
        k_sb = arena.alloc([S], BF16, "k_sb")
        NBmax = S // 128
        Vs = [arena.alloc([NBmax, 128], BF16, f"Vs{i}") for i in range(2)]
        Oacc = arena.alloc([S], F32, "Oacc")
        Dacc = arena.alloc([S], F32, "Dacc")
        mix_sb = arena.alloc([S], BF16, "mix_sb")
        Bt = [arena.alloc([256], F32, f"Bt{i}") for i in range(2)]
        tmpb = [arena.alloc([256], F32, f"tmp{i}") for i in range(3)]
        PT = [arena.alloc([256], BF16, f"PT{i}") for i in range(6)]
        esink = arena.alloc([1], F32, "esink")
        rden = [arena.alloc([128], F32, f"rden{i}") for i in range(2)]
        PS_S = [PS[0], PS[1], PS[2]]
        PS_O = [PS[3], PS[4]]
        PS_D = [PS[5], PS[6]]
        inv_a = 1.0 / math.sqrt(128.0)
        inv_b = 1.0 / math.sqrt(64.0)
        cnt = {"s": 0, "o": 0, "bt": 0, "tmp": 0, "vs": 0}

        for h in range(HA):
            slope = float(cfg.slopes[HB + h])
            sch.op("sp", lambda E, h=h: E.dma_start(out=q_sb.ap, in_=T.qkT_d[h * 128:(h + 1) * 128, :]), writes=[q_sb], dma=True)
            sch.op("sp", lambda E, h=h: E.dma_start(out=k_sb.ap, in_=T.qkT_d[WA + h * 128:WA + (h + 1) * 128, :]), writes=[k_sb], dma=True)
            for bi, d in enumerate((1, 4, 16)):
                cval = np.float32(slope) * np.float32(d)
                B = Bt[cnt["bt"] % 2]; cnt["bt"] += 1
                sch.op("dve", lambda E, B=B, cval=cval: E.scalar_tensor_tensor(out=B.ap, in0=DD, scalar=-float(cval), in1=MA,
                                                                               op0=ALU.mult, op1=ALU.add),
                       reads=[CST], writes=[B])
                L = S // d
                nb = L // 128
                for rho in range(d):
                    V = Vs[cnt["vs"] % 2]; cnt["vs"] += 1
                    vsrc = T.v_d[:, h * 128:(h + 1) * 128]
                    for j0 in range(0, nb, 16):
                        j1 = min(nb, j0 + 16)
                        src = vsrc[rho + j0 * 128 * d: rho + (j1 * 128 - 1) * d + 1: d, :].rearrange("(j p) e -> p j e", p=128)
                        sch.op("sp", lambda E, V=V, j0=j0, j1=j1, src=src: E.dma_start(out=V.ap[:, j0:j1, :], in_=src),
                               writes=[V.sub(j0)], dma=True)
                    qv = q_sb.ap[:, rho::d] if d > 1 else q_sb.ap
                    kv = k_sb.ap[:, rho::d] if d > 1 else k_sb.ap
                    Ov = Oacc.ap[:, rho::d] if d > 1 else Oacc.ap
                    Dv = Dacc.ap[:, rho::d] if d > 1 else Dacc.ap
                    prevPT = None
                    for j in range(nb):
                        nq = 256 if j + 1 < nb else 128
                        P = PS_S[cnt["s"] % 3]
                        TM = tmpb[cnt["s"] % 3]
                        Pt = PT[cnt["s"] % 6]; cnt["s"] += 1
                        sch.op("pe", lambda E, P=P, kv=kv, qv=qv, j=j, nq=nq: E.matmul(
                            P.ap[:, 0:nq], lhsT=kv[:, j * 128:(j + 1) * 128], rhs=qv[:, j * 128:j * 128 + nq], start=True, stop=True),
                            reads=[k_sb, q_sb], writes=[P])
                        sch.op("dve", lambda E, P=P, TM=TM, B=B, nq=nq: E.scalar_tensor_tensor(
                            out=TM.ap[:, 0:nq], in0=P.ap[:, 0:nq], scalar=inv_a, in1=B.ap[:, 0:nq], op0=ALU.mult, op1=ALU.add),
                            reads=[P, B], writes=[TM])
                        sch.op("act", lambda E, TM=TM, Pt=Pt, nq=nq: E.activation(out=Pt.ap[:, 0:nq], in_=TM.ap[:, 0:nq], func=AF.Exp),
                               reads=[TM], writes=[Pt])
                        PO = PS_O[cnt["o"] % 2]; PD = PS_D[cnt["o"] % 2]; cnt["o"] += 1
                        jsub = (j // 16) * 16
                        if j > 0:
                            psub = ((j - 1) // 16) * 16
                            pp = prevPT
                            sch.op("pe", lambda E, PO=PO, V=V, pp=pp, j=j: E.matmul(PO.ap[:, 0:128], lhsT=V.ap[:, j - 1, :], rhs=pp.ap[:, 128:256],
                                                                                  start=True, stop=False),
                                   reads=[V.sub(psub), pp], writes=[PO])
                            sch.op("pe", lambda E, PD=PD, pp=pp: E.matmul(PD.ap[:, 0:128], lhsT=ones_b, rhs=pp.ap[:, 128:256], start=True, stop=False),
                                   reads=[CSTB, pp], writes=[PD])
                        sch.op("pe", lambda E, PO=PO, V=V, Pt=Pt, j=j: E.matmul(PO.ap[:, 0:128], lhsT=V.ap[:, j, :], rhs=Pt.ap[:, 0:128],
                                                                              start=(j == 0), stop=True),
                               reads=[V.sub(jsub), Pt], writes=[PO])
                        sch.op("pe", lambda E, PD=PD, Pt=Pt, j=j: E.matmul(PD.ap[:, 0:128], lhsT=ones_b, rhs=Pt.ap[:, 0:128], start=(j == 0), stop=True),
                               reads=[CSTB, Pt], writes=[PD])
                        osl = Ov[:, j * 128:(j + 1) * 128]
                        dsl = Dv[:, j * 128:(j + 1) * 128]
                        if bi == 0:
                            sch.op("act", lambda E, PO=PO, osl=osl: E.copy(out=osl, in_=PO.ap[:, 0:128]), reads=[PO], writes=[Oacc.sub(0)])
                            sch.op("dve", lambda E, PD=PD, dsl=dsl: E.tensor_copy(out=dsl, in_=PD.ap[:, 0:128]), reads=[PD], writes=[Dacc.sub(0)])
                        else:
                            sch.op("dve", lambda E, PO=PO, osl=osl: E.tensor_tensor(out=osl, in0=PO.ap[:, 0:128], in1=osl, op=ALU.add),
                                   reads=[PO, Oacc.sub(0)], writes=[Oacc.sub(0)])
                            sch.op("dve", lambda E, PD=PD, dsl=dsl: E.tensor_tensor(out=dsl, in0=PD.ap[:, 0:128], in1=dsl, op=ALU.add),
                                   reads=[PD, Dacc.sub(0)], writes=[Dacc.sub(0)])
                        prevPT = Pt
            for s0 in range(0, S, 2048):
                s1 = min(S, s0 + 2048)
                sch.op("dve", lambda E, s0=s0, s1=s1: E.reciprocal(out=Dacc.ap[:, s0:s1], in_=Dacc.ap[:, s0:s1]), reads=[Dacc], writes=[Dacc])
                sch.op("pool", lambda E, s0=s0, s1=s1: E.tensor_tensor(out=mix_sb.ap[:, s0:s1], in0=Oacc.ap[:, s0:s1], in1=Dacc.ap[:, s0:s1], op=ALU.mult),
                       reads=[Oacc, Dacc], writes=[mix_sb])
            sch.op("sp", lambda E, h=h: E.dma_start(out=mixv[:, :, h, :], in_=mix_sb.ap.rearrange("p (t s) -> p t s", s=512)), reads=[mix_sb], dma=True)

        Vz = [Vs[0], Vs[1]]
        nb = S // 128
        cur_kv = -1
        for pr in range(HB // 2):
            kvh = (2 * pr) // 8
            if kvh != cur_kv:
                cur_kv = kvh
                sch.op("pool", lambda E: E.memset(Vz[0].ap, 0.0), writes=[Vz[0]])
                sch.op("pool", lambda E: E.memset(Vz[1].ap, 0.0), writes=[Vz[1]])
                vsrc = T.v_d[:, WA + kvh * 64:WA + (kvh + 1) * 64]
                for j0 in range(0, nb, 16):
                    j1 = min(nb, j0 + 16)
                    src = vsrc[j0 * 128:j1 * 128, :].rearrange("(j p) e -> p j e", p=128)
                    sch.op("sp", lambda E, j0=j0, j1=j1, src=src: E.dma_start(out=Vz[0].ap[:, j0:j1, 0:64], in_=src), writes=[Vz[0]], dma=True)
                    sch.op("sp", lambda E, j0=j0, j1=j1, src=src: E.dma_start(out=Vz[1].ap[:, j0:j1, 64:128], in_=src), writes=[Vz[1]], dma=True)
                kr = 2 * WA + WB + kvh * 64
                sch.op("sp", lambda E, kr=kr: E.dma_start(out=k_sb.ap[0:64, :], in_=T.qkT_d[kr:kr + 64, :]), writes=[k_sb], dma=True)
                sch.op("sp", lambda E, kr=kr: E.dma_start(out=k_sb.ap[64:128, :], in_=T.qkT_d[kr:kr + 64, :]), writes=[k_sb], dma=True)
            qr = 2 * WA + pr * 128
            sch.op("sp", lambda E, qr=qr: E.dma_start(out=q_sb.ap, in_=T.qkT_d[qr:qr + 128, :]), writes=[q_sb], dma=True)
            srow = l * (HB // 2) + pr
            sch.op("sp", lambda E, srow=srow: E.dma_start(out=esink.ap, in_=T.sinks[srow:srow + 1, :].rearrange("o p -> p o")), writes=[esink], dma=True)
            sch.op("act", lambda E: E.activation(out=esink.ap, in_=esink.ap, func=AF.Exp), reads=[esink], writes=[esink])
            Bh = []
            for hh in range(2):
                slope = float(cfg.slopes[2 * pr + hh])
                B = Bt[hh]
                sch.op("dve", lambda E, B=B, slope=slope: E.scalar_tensor_tensor(out=B.ap, in0=DD, scalar=-slope, in1=MB, op0=ALU.mult, op1=ALU.add),
                       reads=[CST], writes=[B])
                Bh.append(B)
            prev = [None, None]
            for j in range(nb):
                nq = 256 if j + 1 < nb else 128
                cur = []
                for hh in range(2):
                    P = PS_S[cnt["s"] % 3]
                    TM = tmpb[cnt["s"] % 3]
                    Pt = PT[cnt["s"] % 6]; cnt["s"] += 1
                    lo, hi = hh * 64, (hh + 1) * 64
                    sch.op("pe", lambda E, P=P, j=j, nq=nq, lo=lo, hi=hi: E.matmul(
                        P.ap[:, 0:nq], lhsT=k_sb.ap[lo:hi, j * 128:(j + 1) * 128], rhs=q_sb.ap[lo:hi, j * 128:j * 128 + nq], start=True, stop=True),
                        reads=[k_sb, q_sb], writes=[P])
                    sch.op("dve", lambda E, P=P, TM=TM, hh=hh, nq=nq: E.scalar_tensor_tensor(
                        out=TM.ap[:, 0:nq], in0=P.ap[:, 0:nq], scalar=inv_b, in1=Bh[hh].ap[:, 0:nq], op0=ALU.mult, op1=ALU.add),
                        reads=[P, Bh[hh]], writes=[TM])
                    sch.op("act", lambda E, TM=TM, Pt=Pt, nq=nq: E.activation(out=Pt.ap[:, 0:nq], in_=TM.ap[:, 0:nq], func=AF.Exp),
                           reads=[TM], writes=[Pt])
                    cur.append(Pt)
                PO = PS_O[cnt["o"] % 2]; PD = PS_D[cnt["o"] % 2]
                R = rden[cnt["o"] % 2]; cnt["o"] += 1
                mms = []
                if j > 0:
                    for hh in range(2):
                        mms.append((Vz[hh], j - 1, prev[hh], 128, 256, hh))
                for hh in range(2):
                    mms.append((Vz[hh], j, cur[hh], 0, 128, hh))
                for mi, (Vb, jj, Pb, a0, a1, hh) in enumerate(mms):
                    first = (mi == 0); last = (mi == len(mms) - 1)
                    sch.op("pe", lambda E, PO=PO, Vb=Vb, jj=jj, Pb=Pb, a0=a0, a1=a1, first=first, last=last: E.matmul(
                        PO.ap[:, 0:128], lhsT=Vb.ap[:, jj, :], rhs=Pb.ap[:, a0:a1], start=first, stop=last),
                        reads=[Vb, Pb], writes=[PO])
                for mi, (Vb, jj, Pb, a0, a1, hh) in enumerate(mms):
                    first = (mi == 0); last = (mi == len(mms) - 1)
                    sch.op("pe", lambda E, PD=PD, Pb=Pb, a0=a0, a1=a1, hh=hh, first=first, last=last: E.matmul(
                        PD.ap[:, 0:128], lhsT=onesz[hh], rhs=Pb.ap[:, a0:a1], start=first, stop=last),
                        reads=[CSTB, Pb], writes=[PD])
                sch.op("dve", lambda E, PD=PD, R=R: E.tensor_scalar(out=R.ap, in0=PD.ap[:, 0:128], scalar1=esink.ap[:, 0:1], scalar2=None, op0=ALU.add),
                       reads=[PD, esink], writes=[R])
                sch.op("dve", lambda E, R=R: E.reciprocal(out=R.ap, in_=R.ap), reads=[R], writes=[R])
                sch.op("dve", lambda E, PO=PO, R=R, j=j: E.tensor_tensor(out=mix_sb.ap[:, j * 128:(j + 1) * 128], in0=PO.ap[:, 0:128], in1=R.ap, op=ALU.mult),
                       reads=[PO, R], writes=[mix_sb.sub(0)])
                prev = cur
            mk = WA // 128 + pr
            sch.op("sp", lambda E, mk=mk: E.dma_start(out=mixv[:, :, mk, :], in_=mix_sb.ap.rearrange("p (t s) -> p t s", s=512)), reads=[mix_sb], dma=True)

    def phase_projT(w_rows_ap, KCn, actT_d, TB):
        new_phase()
        ob = [arena.alloc([512], F32, f"po{i}") for i in range(4)]
        st = {"n": 0}

        def evac(ci, tt, P, c0, cw):
            O = ob[st["n"] % 4]; st["n"] += 1
            if st["n"] % 2 == 0:
                sch.op("act", lambda E: E.copy(out=O.ap[:, 0:cw], in_=P.ap[:, 0:cw]), reads=[P], writes=[O])
            else:
                sch.op("dve", lambda E: E.tensor_copy(out=O.ap[:, 0:cw], in_=P.ap[:, 0:cw]), reads=[P], writes=[O])
            sch.op("sp", lambda E: E.dma_start(out=T.y_d[tt * 128:(tt + 1) * 128, c0:c0 + cw], in_=O.ap[:, 0:cw]), reads=[O], dma=True)

        gemm_T(w_rows_ap, KCn, [(c0, 512) for c0 in range(0, D, 512)], actT_d, TB, evac)

    def phase_ln(l, xres_d, m_gate, g_in, b_in, dst_d):
        new_phase()
        gate = arena.alloc([D], F32, "gate")
        gbc = arena.alloc([D], F32, "gbc")
        bbc = arena.alloc([D], F32, "bbc")
        sch.op("sp", lambda E: E.dma_start(out=gate.ap, in_=T.mod_d[l:l + 1, m_gate * D:(m_gate + 1) * D].partition_broadcast(128)), writes=[gate], dma=True)
        sch.op("sp", lambda E: E.dma_start(out=gbc.ap, in_=g_in[l:l + 1, :].partition_broadcast(128)), writes=[gbc], dma=True)
        sch.op("sp", lambda E: E.dma_start(out=bbc.ap, in_=b_in[l:l + 1, :].partition_broadcast(128)), writes=[bbc], dma=True)
        sch.op("pool", lambda E: E.tensor_scalar(out=gate.ap, in0=gate.ap, scalar1=1.0, scalar2=None, op0=ALU.add), reads=[gate], writes=[gate])
        xb = [arena.alloc([D], F32, f"lx{i}") for i in range(2)]
        yb = [arena.alloc([D], F32, f"ly{i}") for i in range(2)]
        zb = [arena.alloc([D], F32, f"lz{i}") for i in range(2)]
        stat = [arena.alloc([8], F32, f"st{i}") for i in range(2)]
        for tt in range(NT):
            X = xb[tt % 2]; Y = yb[tt % 2]; Z = zb[tt % 2]; St = stat[tt % 2]
            sch.op("sp", lambda E, X=X, tt=tt: E.dma_start(out=X.ap, in_=xres_d[tt * 128:(tt + 1) * 128, :]), writes=[X], dma=True)
            sch.op("sp", lambda E, Y=Y, tt=tt: E.dma_start(out=Y.ap, in_=T.y_d[tt * 128:(tt + 1) * 128, :]), writes=[Y], dma=True)
            sch.op("pool", lambda E, Y=Y: E.tensor_tensor(out=Y.ap, in0=Y.ap, in1=gate.ap, op=ALU.mult), reads=[Y, gate], writes=[Y])
            sch.op("dve", lambda E, X=X, Y=Y, Z=Z, St=St: E.scalar_tensor_tensor(out=Z.ap, in0=X.ap, scalar=float(cfg.alpha), in1=Y.ap,
                                                                                 op0=ALU.mult, op1=ALU.add, accum_out=St.ap[:, 0:1]),
                   reads=[X, Y], writes=[Z, St])
            sch.op("dve", lambda E, St=St: E.tensor_scalar(out=St.ap[:, 1:2], in0=St.ap[:, 0:1], scalar1=1.0 / D, scalar2=None, op0=ALU.mult),
                   reads=[St], writes=[St])
            sch.op("dve", lambda E, Z=Z, St=St: E.tensor_scalar(out=Z.ap, in0=Z.ap, scalar1=St.ap[:, 1:2], scalar2=None, op0=ALU.subtract),
                   reads=[Z, St], writes=[Z])
            sch.op("act", lambda E, Z=Z, X=X, St=St: E.activation(out=X.ap, in_=Z.ap, func=AF.Square, accum_out=St.ap[:, 2:3]),
                   reads=[Z], writes=[X, St])
            sch.op("dve", lambda E, St=St: E.tensor_scalar(out=St.ap[:, 3:4], in0=St.ap[:, 2:3], scalar1=1.0 / D, scalar2=LN_EPS, op0=ALU.mult, op1=ALU.add),
                   reads=[St], writes=[St])
            sch.op("act", lambda E, St=St: E.activation(out=St.ap[:, 4:5], in_=St.ap[:, 3:4], func=AF.Sqrt), reads=[St], writes=[St])
            sch.op("dve", lambda E, St=St: E.reciprocal(out=St.ap[:, 5:6], in_=St.ap[:, 4:5]), reads=[St], writes=[St])
            sch.op("dve", lambda E, Z=Z, St=St: E.scalar_tensor_tensor(out=Z.ap, in0=Z.ap, scalar=St.ap[:, 5:6], in1=gbc.ap, op0=ALU.mult, op1=ALU.mult),
                   reads=[Z, St, gbc], writes=[Z])
            sch.op("pool", lambda E, Z=Z: E.tensor_tensor(out=Z.ap, in0=Z.ap, in1=bbc.ap, op=ALU.add), reads=[Z, bbc], writes=[Z])
            sch.op("sp", lambda E, Z=Z, tt=tt: E.dma_start(out=dst_d[tt * 128:(tt + 1) * 128, :], in_=Z.ap), reads=[Z], dma=True)

    def phase_up(l):
        new_phase()
        wl = T.w_up[l * D:(l + 1) * D, :]
        CP = 2
        assert FC % CP == 0
        cwt = arena.alloc([3, 128], F32, "cwt")
        cwp = [arena.alloc([4], F32, f"cwp{i}") for i in range(2 * CP * 2)]
        carry = [arena.alloc([2], F32, f"carry{i}") for i in range(2 * CP)]
        acc = [arena.alloc([512], F32, f"acc{i}") for i in range(2 * CP * 2)]
        aout = [arena.alloc([CP, 512], BF16, f"aout{i}") for i in range(2)]
        rowt = [arena.alloc([128], F32, f"rowt{i}") for i in range(2)]
        wv = wl.rearrange("(k p) c -> p k c", p=128)
        aTv = T.aT_d.rearrange("(t p) (k s) -> p t k s", p=128, k=FC)
        wbuf = [arena.alloc([KC, 2 * CP * 128], BF16, f"uw{i}") for i in range(2)]
        abuf = [arena.alloc([KC, 512], BF16, f"ua{i}") for i in range(2)]
        ai = 0
        pi = 0
        ao = 0
        for gi in range(FC // CP):
            W = wbuf[gi % 2]
            cols = []
            for i in range(CP):
                cols.append((gi * CP + i) * 128)
            for i in range(CP):
                cols.append(DFF + (gi * CP + i) * 128)
            for half in range(2):
                c0h = cols[half * CP]
                sch.op("pool", lambda E, W=W, half=half, c0h=c0h: E.dma_start(out=W.ap[:, :, half * CP * 128:(half + 1) * CP * 128],
                                                                           in_=wv[:, :, c0h:c0h + CP * 128]),
                       writes=[W.sub(half * CP + i_) for i_ in range(CP)], dma=True)
            for ti, c0 in enumerate(cols):
                RT = rowt[(gi * 2 * CP + ti) % 2]
                CW = cwp[(gi % 2) * 2 * CP + ti]
                sch.op("sp", lambda E, RT=RT, c0=c0: E.dma_start(out=RT.ap[0:3, :], in_=T.conv_w[l * 3:(l + 1) * 3, c0:c0 + 128]), writes=[RT], dma=True)
                sch.op("sp", lambda E, RT=RT, c0=c0: E.dma_start(out=RT.ap[3:4, :], in_=T.conv_b[l:l + 1, c0:c0 + 128]), writes=[RT], dma=True)
                PX = PS[pi % 8]; pi += 1
                sch.op("pe", lambda E, PX=PX, RT=RT: E.transpose(out=PX.ap[:, 0:4], in_=RT.ap[0:4, :], identity=ident[0:4, 0:4]),
                       reads=[RT, CST], writes=[PX])
                sch.op("dve", lambda E, PX=PX, CW=CW: E.tensor_copy(out=CW.ap, in_=PX.ap[:, 0:4]), reads=[PX], writes=[CW])
                sch.op("pool", lambda E, ti=ti: E.memset(carry[ti].ap, 0.0), writes=[carry[ti]])
            for tb in range(NB512):
                A = abuf[ai % 2]; ai += 1
                sch.op("sp", lambda E, A=A, tb=tb: E.dma_start(out=A.ap, in_=blk(T.uT_d, KC, tb)), writes=[A], dma=True)
                accs = []
                for ti in range(2 * CP):
                    P = PS[pi % 8]; pi += 1
                    for k in range(KC):
                        sch.op("pe", lambda E, P=P, W=W, A=A, k=k, ti=ti: E.matmul(
                            P.ap[:, :], lhsT=W.ap[:, k, ti * 128:(ti + 1) * 128], rhs=A.ap[:, k, :], start=(k == 0), stop=(k == KC - 1)),
                            reads=[W.sub(ti), A], writes=[P])
                    CW = cwp[(gi % 2) * 2 * CP + ti]
                    AC = acc[(tb % 2) * 2 * CP + ti]
                    CA = carry[ti]
                    sch.op("act", lambda E, P=P, AC=AC, CW=CW: E.activation(out=AC.ap, in_=P.ap, func=AF.Identity, scale=CW.ap[:, 2:3], bias=CW.ap[:, 3:4]),
                           reads=[P, CW], writes=[AC])
                    sch.op("dve", lambda E, P=P, AC=AC, CW=CW: E.scalar_tensor_tensor(out=AC.ap[:, 1:512], in0=P.ap[:, 0:511], scalar=CW.ap[:, 1:2],
                                                                                      in1=AC.ap[:, 1:512], op0=ALU.mult, op1=ALU.add),
                           reads=[P, CW, AC], writes=[AC])
                    sch.op("dve", lambda E, P=P, AC=AC, CW=CW: E.scalar_tensor_tensor(out=AC.ap[:, 2:512], in0=P.ap[:, 0:510], scalar=CW.ap[:, 0:1],
                                                                                      in1=AC.ap[:, 2:512], op0=ALU.mult, op1=ALU.add),
                           reads=[P, CW, AC], writes=[AC])
                    sch.op("dve", lambda E, AC=AC, CW=CW, CA=CA: E.scalar_tensor_tensor(out=AC.ap[:, 0:1], in0=CA.ap[:, 1:2], scalar=CW.ap[:, 1:2],
                                                                                        in1=AC.ap[:, 0:1], op0=ALU.mult, op1=ALU.add),
                           reads=[CA, CW, AC], writes=[AC])
                    sch.op("dve", lambda E, AC=AC, CW=CW, CA=CA: E.scalar_tensor_tensor(out=AC.ap[:, 0:2], in0=CA.ap[:, 0:2], scalar=CW.ap[:, 0:1],
                                                                                        in1=AC.ap[:, 0:2], op0=ALU.mult, op1=ALU.add),
                           reads=[CA, CW, AC], writes=[AC])
                    sch.op("dve", lambda E, P=P, CA=CA: E.tensor_copy(out=CA.ap, in_=P.ap[:, 510:512]), reads=[P], writes=[CA])
                    accs.append(AC)
                AO = aout[ao % 2]; ao += 1
                for i in range(CP):
                    G_ = accs[i]; V_ = accs[CP + i]
                    sch.op("act", lambda E, G_=G_: E.activation(out=G_.ap, in_=G_.ap, func=AF.Silu), reads=[G_], writes=[G_])
                    sch.op("pool", lambda E, G_=G_, V_=V_, AO=AO, i=i: E.tensor_tensor(out=AO.ap[:, i, :], in0=G_.ap, in1=V_.ap, op=ALU.mult),
                           reads=[G_, V_], writes=[AO.sub(i)])
                r0 = gi * CP * 128
                k0 = gi * CP
                for i in range(CP):
                    dst = aTv[:, tb * 4:(tb + 1) * 4, k0 + i, :]
                    sch.op("sp", lambda E, AO=AO, dst=dst, i=i: E.dma_start(out=dst, in_=AO.ap[:, i, :].rearrange("p (j s) -> p j s", j=4)),
                           reads=[AO.sub(i)], dma=True)

    for (ph, l) in plan["phases"]:
        if ph == "mod":
            phase_mod()
        elif ph == "A":
            cur = T.x if (l == 0 or plan.get("layer_local")) else T.x2_d
            phase_u(l, cur, 0, 1)
            phase_inproj(l)
            phase_vproj(l)
            phase_attn(l)
            phase_projT(T.w_out[l * D:(l + 1) * D, :], KC, T.mixT_d, 512)
            phase_ln(l, cur, 2, T.ln1_g, T.ln1_b, T.x1_d)
        elif ph == "B":
            phase_u(l, T.x1_d, 3, 4)
            phase_up(l)
            phase_projT(T.w_down[l * DFF:(l + 1) * DFF, :], FC, T.aT_d, 128)
            phase_ln(l, T.x1_d, 5, T.ln2_g, T.ln2_b, T.x2_d)
    sch.fence()
    with nc.Block() as block:
        sch.emit(block)
    return nc, used_inputs


def make_consts():
    k = np.arange(128, dtype=np.float32)[:, None]
    q = np.arange(128, dtype=np.float32)[None, :]
    ddiag = q - k
    dprev = 128.0 + q - k
    DD = np.concatenate([ddiag, dprev], axis=1)
    madiag = np.where(ddiag >= 0, 0.0, NEG)
    MA = np.concatenate([madiag, np.where(dprev <= 128, 0.0, NEG)], axis=1)
    MB = np.concatenate([madiag, np.where(dprev <= 127, 0.0, NEG)], axis=1)
    DDm = DD.copy()
    ident = np.eye(128, dtype=np.float32)
    return np.ascontiguousarray(np.concatenate([ident, DDm, MA, MB], axis=1).astype(np.float32))


def _f(a):
    return np.ascontiguousarray(np.asarray(a, dtype=np.float32))


def host_arrays(cfg, inputs):
    S, D, DEPTH = cfg.S, cfg.D, cfg.DEPTH
    sk = _f(inputs["sinks"]).reshape(DEPTH, cfg.HB // 2, 2)
    sk = np.ascontiguousarray(np.repeat(sk, 64, axis=2))
    return {
        "x": _f(inputs["x"]).reshape(S, D),
        "c": _f(inputs["c"]).reshape(cfg.KC, 128),
        "w_mod": _f(inputs["w_mod"]), "b_mod": _f(inputs["b_mod"]),
        "w_in": _f(inputs["w_in"]), "sinks": sk, "w_out": _f(inputs["w_out"]),
        "ln1_g": _f(inputs["ln1_g"]), "ln1_b": _f(inputs["ln1_b"]),
        "w_up": _f(inputs["w_up"]), "conv_w": _f(inputs["conv_w"]), "conv_b": _f(inputs["conv_b"]),
        "w_down": _f(inputs["w_down"]), "ln2_g": _f(inputs["ln2_g"]), "ln2_b": _f(inputs["ln2_b"]),
    }


def layer_slice(name, arr, l0, l1):
    a = arr[l0:l1]
    if a.ndim == 3:
        return np.ascontiguousarray(a.reshape(a.shape[0] * a.shape[1], a.shape[2]))
    return np.ascontiguousarray(a)


_PROG_CACHE = {}


def launch(cfg, plan, host, l0, l1, extra, debug=False, trace=False):
    key = (cfg.S, cfg.D, tuple(plan["phases"]), plan.get("depth", 1), debug)
    if key not in _PROG_CACHE:
        _PROG_CACHE[key] = build_program(cfg, plan, debug=debug)
    nc, used = _PROG_CACHE[key]
    im = {}
    for n in used:
        if n in extra:
            im[n] = extra[n]
        elif n == "consts":
            im[n] = make_consts()
        elif n in ("x", "c"):
            im[n] = host[n]
        else:
            im[n] = layer_slice(n, host[n], l0, l1)
    res = run_bass_kernel_spmd(nc, [im], core_ids=[0], trace=trace)
    return res.results[0]


def run_fused(cfg, inputs, debug=False):
    host = host_arrays(cfg, inputs)
    phases = [("mod", 0)]
    for l in range(cfg.DEPTH):
        phases += [("A", l), ("B", l)]
    plan = dict(phases=phases, depth=cfg.DEPTH, ext_out={"x2_d"})
    r = launch(cfg, plan, host, 0, cfg.DEPTH, {}, debug=debug)
    return r


def run_multi(cfg, inputs):
    host = host_arrays(cfg, inputs)
    r = launch(cfg, dict(phases=[("mod", 0)], depth=cfg.DEPTH, ext_out={"mod_d"}), host, 0, cfg.DEPTH, {})
    mod = np.asarray(r["mod_d"], dtype=np.float32)
    x = host["x"]
    planA = dict(phases=[("A", 0)], depth=1, layer_local=True, ext_in={"mod_d"}, ext_out={"x1_d"})
    planB = dict(phases=[("B", 0)], depth=1, layer_local=True, ext_in={"mod_d", "x1_d"}, ext_out={"x2_d"})
    for l in range(cfg.DEPTH):
        ml = np.ascontiguousarray(mod[l:l + 1])
        r = launch(cfg, planA, host, l, l + 1, {"mod_d": ml, "x": x})
        x1 = np.ascontiguousarray(np.asarray(r["x1_d"], dtype=np.float32))
        r = launch(cfg, planB, host, l, l + 1, {"mod_d": ml, "x1_d": x1})
        x = np.ascontiguousarray(np.asarray(r["x2_d"], dtype=np.float32))
    return x


def kernel(**inputs):
    cfg = Cfg()
    o = run_multi(cfg, inputs)
    return np.asarray(o, dtype=np.float32).reshape(1, cfg.S, cfg.D)
```

```python
import math
import numpy as np
import concourse.bass as bass
import concourse.mybir as mybir
from concourse.bass_utils import run_bass_kernel_spmd

F32 = mybir.dt.float32
BF16 = mybir.dt.bfloat16
AF = mybir.ActivationFunctionType
ALU = mybir.AluOpType
NEG = -30000.0
LN_EPS = 1e-5


class Cfg:
    def __init__(self, S=8192, D=4096, DEPTH=2):
        self.S, self.D, self.DEPTH = S, D, DEPTH
        self.KC = D // 128
        self.HA = (D // 2) // 128
        self.HB = (D // 2) // 64
        self.HKV = self.HB // 8
        self.WA = self.HA * 128
        self.WB = self.HB * 64
        self.KVB = self.HKV * 64
        self.INW = 3 * self.WA + self.WB + 2 * self.KVB
        self.DFF = 256 * math.ceil(8 * D / 3 / 256)
        self.FC = self.DFF // 128
        self.QKW = 2 * self.WA + self.WB + self.KVB
        self.VW = self.WA + self.KVB
        self.alpha = (2 * DEPTH) ** 0.25
        n_al = self.HA + self.HB
        i = np.arange(1, n_al + 1, dtype=np.float32)
        self.slopes = np.exp2(np.float32(-8.0) * i / np.float32(n_al)).astype(np.float32)


class Buf:
    def __init__(self, ap, name=""):
        self.ap = ap
        self.name = name
        self.subs = {}

    def sub(self, i):
        if i not in self.subs:
            self.subs[i] = ("sub", id(self), i)
        return (self, i)


class Sched:
    ENG = ["pe", "act", "dve", "pool", "sp"]

    def __init__(self, nc, ndma=6):
        self.nc = nc
        self.ops = {e: [] for e in self.ENG}
        self.needed = {e: set() for e in self.ENG}
        self.csem = {e: nc.alloc_semaphore(name=f"c_{e}") for e in self.ENG}
        self.dsem = {e: [nc.alloc_semaphore(name=f"d_{e}{i}") for i in range(ndma)] for e in ("sp", "pool", "act")}
        self.dcnt = {(e, i): 0 for e in ("sp", "pool", "act") for i in range(ndma)}
        self.drr = {e: 0 for e in ("sp", "pool", "act")}
        self.ndma = ndma
        self.last_w = {}
        self.readers = {}
        self.fence_tokens = []

    def _keys(self, k):
        if isinstance(k, tuple):
            b, i = k
            return [b.subs[i]], [("whole", id(b))]
        ks = [("whole", id(k))] + list(k.subs.values())
        return ks, []

    @staticmethod
    def _stream(tok):
        return (tok[0], tok[1]) if tok[0] == "c" else (tok[0], tok[1], tok[2])

    def _mark(self, tok):
        if tok[0] == "c":
            self.needed[tok[1]].add(tok[2])

    def op(self, eng, fn, reads=(), writes=(), dma=False):
        deps = {}

        def add(tok):
            if tok[0] == "c" and tok[1] == "pe" and eng == "pe" and not dma:
                return
            s = self._stream(tok)
            if s not in deps or deps[s][-1] < tok[-1]:
                deps[s] = tok

        for t in self.fence_tokens:
            add(t)
        for k in reads:
            own, chk = self._keys(k)
            for kk in own + chk:
                if kk in self.last_w:
                    add(self.last_w[kk])
        for k in writes:
            own, chk = self._keys(k)
            for kk in own + chk:
                if kk in self.last_w:
                    add(self.last_w[kk])
                for t in self.readers.get(kk, {}).values():
                    add(t)
        if dma:
            slot = self.drr[eng]
            self.drr[eng] = (slot + 1) % self.ndma
            prev = self.dcnt[(eng, slot)]
            if prev > 0:
                add(("d", eng, slot, prev))
            val = prev + 16
            self.dcnt[(eng, slot)] = val
            tok = ("d", eng, slot, val)
        else:
            tok = ("c", eng, len(self.ops[eng]))
        dl = list(deps.values())
        for t in dl:
            self._mark(t)
        self.ops[eng].append((fn, dl, tok))
        for k in writes:
            own, _ = self._keys(k)
            for kk in own:
                self.last_w[kk] = tok
                self.readers[kk] = {}
        for k in reads:
            own, _ = self._keys(k)
            for kk in own:
                self.readers.setdefault(kk, {})[self._stream(tok)] = tok
        return tok

    def fence(self):
        toks = []
        for e in self.ENG:
            for idx in range(len(self.ops[e]) - 1, -1, -1):
                if self.ops[e][idx][2][0] == "c":
                    t = ("c", e, idx)
                    toks.append(t)
                    self._mark(t)
                    break
        for (e, i), v in self.dcnt.items():
            if v > 0:
                toks.append(("d", e, i, v))
        self.fence_tokens = toks
        self.last_w = {}
        self.readers = {}

    def emit(self, block):
        cval = {}
        for e in self.ENG:
            m = {}
            c = 0
            for idx, (fn, dl, tok) in enumerate(self.ops[e]):
                if tok[0] == "c" and idx in self.needed[e]:
                    c += 1
                    m[idx] = c
            cval[e] = m
        final_dma = dict(self.dcnt)
        sched = self

        def run(e, E):
            waited = {}

            def wait(sem, key, val):
                if waited.get(key, 0) < val:
                    E.wait_ge(sem, val)
                    waited[key] = val

            for idx, (fn, dl, tok) in enumerate(sched.ops[e]):
                for d in dl:
                    if d[0] == "c":
                        wait(sched.csem[d[1]], ("c", d[1]), cval[d[1]][d[2]])
                    else:
                        wait(sched.dsem[d[1]][d[2]], ("d", d[1], d[2]), d[3])
                ins = fn(E)
                if tok[0] == "d":
                    ins.then_inc(sched.dsem[tok[1]][tok[2]], 16)
                elif idx in sched.needed[e]:
                    ins.then_inc(sched.csem[e], 1)
            if e == "sp":
                for (de, i), v in final_dma.items():
                    if v > 0:
                        wait(sched.dsem[de][i], ("d", de, i), v)

        @block.sync
        def _(E):
            run("sp", E)

        @block.tensor
        def _(E):
            run("pe", E)

        @block.scalar
        def _(E):
            run("act", E)

        @block.vector
        def _(E):
            run("dve", E)

        @block.gpsimd
        def _(E):
            run("pool", E)


class Arena:
    def __init__(self, nc, nbytes):
        self.t = nc.alloc_sbuf_tensor("arena", [128, nbytes // 4], F32)
        self.nbytes = nbytes
        self.off = 0

    def reset(self):
        self.off = 0

    def alloc(self, free_shape, dtype, name=""):
        esz = 2 if dtype == BF16 else 4
        n = int(np.prod(free_shape))
        nb = (n * esz + 31) // 32 * 32
        assert self.off + nb <= self.nbytes, f"arena overflow {name} {self.off}+{nb}>{self.nbytes}"
        w0 = self.off // 4
        ap = self.t[:, w0:w0 + nb // 4]
        self.off += nb
        if dtype != F32:
            ap = ap.bitcast(dtype)
        ap = ap[:, 0:n]
        if len(free_shape) == 2:
            ap = ap.rearrange("p (a b) -> p a b", a=free_shape[0], b=free_shape[1])
        elif len(free_shape) == 3:
            ap = ap.rearrange("p (a b c) -> p a b c", a=free_shape[0], b=free_shape[1], c=free_shape[2])
        return Buf(ap, name)


def build_program(cfg, plan, debug=False):
    S, D, KC = cfg.S, cfg.D, cfg.KC
    DEPTH = plan.get("depth", 1)
    HA, HB, HKV, WA, WB, KVB = cfg.HA, cfg.HB, cfg.HKV, cfg.WA, cfg.WB, cfg.KVB
    DFF, FC, QKW, VW = cfg.DFF, cfg.FC, cfg.QKW, cfg.VW
    NT = S // 128
    NB512 = S // 512
    nc = bass.Bass("TRN2", target_bir_lowering=False)
    used_inputs = []

    in_specs = {
        "x": [S, D], "c": [KC, 128], "w_mod": [DEPTH * D, 6 * D], "b_mod": [DEPTH, 6 * D],
        "w_in": [DEPTH * D, cfg.INW], "sinks": [DEPTH * HB // 2, 128], "w_out": [DEPTH * D, D],
        "ln1_g": [DEPTH, D], "ln1_b": [DEPTH, D], "w_up": [DEPTH * D, 2 * DFF],
        "conv_w": [DEPTH * 3, 2 * DFF], "conv_b": [DEPTH, 2 * DFF], "w_down": [DEPTH * DFF, D],
        "ln2_g": [DEPTH, D], "ln2_b": [DEPTH, D], "consts": [128, 128 + 3 * 256],
    }
    sc_specs = {
        "mod_d": ([DEPTH, 6 * D], F32), "uT_d": ([NB512 * 128, KC * 512], BF16), "qkT_d": ([QKW, S], BF16), "v_d": ([S, VW], BF16),
        "mixT_d": ([NB512 * 128, KC * 512], BF16), "y_d": ([S, D], F32), "x1_d": ([S, D], F32), "x2_d": ([S, D], F32),
        "aT_d": ([NT * 128, FC * 128], BF16),
    }
    cache = {}

    class _T:
        def __getattr__(self, name):
            if name in cache:
                return cache[name]
            if name in in_specs:
                ap = nc.dram_tensor(name, list(in_specs[name]), F32, kind="ExternalInput").ap()
                used_inputs.append(name)
            else:
                shape, dt = sc_specs[name]
                if name in plan.get("ext_in", ()):
                    ap = nc.dram_tensor(name, list(shape), dt, kind="ExternalInput").ap()
                    used_inputs.append(name)
                elif name in plan.get("ext_out", ()) or debug:
                    ap = nc.dram_tensor(name, list(shape), dt, kind="ExternalOutput").ap()
                else:
                    ap = nc.dram_tensor(name, list(shape), dt).ap()
            cache[name] = ap
            return ap

    T = _T()

    def blk(act_d, KCn, tb):
        return act_d[tb * 128:(tb + 1) * 128, :].rearrange("p (k s) -> p k s", k=KCn)

    sch = Sched(nc)
    arena = Arena(nc, 176 * 1024)
    cst_t = nc.alloc_sbuf_tensor("cst", [128, 128 + 3 * 256], F32)
    cstb_t = nc.alloc_sbuf_tensor("cstb", [128, 3 * 128], BF16)
    CST = Buf(cst_t[:], "cst")
    CSTB = Buf(cstb_t[:], "cstb")
    ident = cst_t[:, 0:128]
    DD = cst_t[:, 128:384]
    MA = cst_t[:, 384:640]
    MB = cst_t[:, 640:896]
    ones_b = cstb_t[:, 0:128]
    onesz = [cstb_t[:, 128:256], cstb_t[:, 256:384]]

    psum_ctx = [nc.psum_tensor(f"ps{i}", [128, 512], F32) for i in range(8)]
    ps_t = [c.__enter__() for c in psum_ctx]
    PS = [Buf(t[:], f"ps{i}") for i, t in enumerate(ps_t)]

    sch.op("sp", lambda E: E.dma_start(out=cst_t[:], in_=T.consts), writes=[CST], dma=True)
    sch.op("pool", lambda E: E.memset(cstb_t[:, 0:128], 1.0), writes=[CSTB])
    sch.op("pool", lambda E: E.memset(cstb_t[:, 128:384], 0.0), writes=[CSTB])
    sch.op("pool", lambda E: E.memset(cstb_t[:, 128:192], 1.0), writes=[CSTB])
    sch.op("pool", lambda E: E.memset(cstb_t[:, 320:384], 1.0), writes=[CSTB])

    def new_phase():
        sch.fence()
        arena.reset()

    def load_pp(src_row_ap, dst, tmp, psb, add_one=False):
        sch.op("sp", lambda E: E.dma_start(out=tmp.ap[0:KC, 0:128], in_=src_row_ap.rearrange("(k p) -> k p", p=128)),
               writes=[tmp], dma=True)
        sch.op("pe", lambda E: E.transpose(out=psb.ap[:, 0:KC], in_=tmp.ap[0:KC, 0:128], identity=ident[0:KC, 0:KC]),
               reads=[tmp, CST], writes=[psb])
        if add_one:
            sch.op("dve", lambda E: E.tensor_scalar(out=dst.ap, in0=psb.ap[:, 0:KC], scalar1=1.0, scalar2=None, op0=ALU.add),
                   reads=[psb], writes=[dst])
        else:
            sch.op("dve", lambda E: E.tensor_copy(out=dst.ap, in_=psb.ap[:, 0:KC]), reads=[psb], writes=[dst])

    def phase_mod():
        new_phase()
        crow = arena.alloc([128], F32, "crow")
        srow = arena.alloc([128], F32, "srow")
        sc = arena.alloc([KC], F32, "sc")
        sch.op("sp", lambda E: E.dma_start(out=crow.ap[0:KC, :], in_=T.c), writes=[crow], dma=True)
        sch.op("act", lambda E: E.activation(out=srow.ap[0:KC, :], in_=crow.ap[0:KC, :], func=AF.Silu),
               reads=[crow], writes=[srow])
        sch.op("pe", lambda E: E.transpose(out=PS[0].ap[:, 0:KC], in_=srow.ap[0:KC, :], identity=ident[0:KC, 0:KC]),
               reads=[srow, CST], writes=[PS[0]])
        sch.op("dve", lambda E: E.tensor_copy(out=sc.ap, in_=PS[0].ap[:, 0:KC]), reads=[PS[0]], writes=[sc])
        wb = [arena.alloc([KC, 512], F32, f"wmod{i}") for i in range(2)]
        bb = [arena.alloc([512], F32, f"bm{i}") for i in range(2)]
        rb = [arena.alloc([512], F32, f"rm{i}") for i in range(2)]
        it = 0
        for l in range(DEPTH):
            wv = T.w_mod[l * D:(l + 1) * D, :].rearrange("(k p) c -> p k c", p=128)
            for n in range(6 * D // 512):
                W = wb[it % 2]; Bm = bb[it % 2]; R = rb[it % 2]; P = PS[1 + it % 2]
                sch.op("sp", lambda E, W=W, n=n, wv=wv: E.dma_start(out=W.ap, in_=wv[:, :, n * 512:(n + 1) * 512]),
                       writes=[W], dma=True)
                sch.op("sp", lambda E, Bm=Bm, n=n, l=l: E.dma_start(out=Bm.ap[0:1, :], in_=T.b_mod[l:l + 1, n * 512:(n + 1) * 512]),
                       writes=[Bm], dma=True)
                for k in range(KC):
                    sch.op("pe", lambda E, W=W, P=P, k=k: E.matmul(P.ap[0:1, :], lhsT=sc.ap[:, k:k + 1], rhs=W.ap[:, k, :],
                                                                  start=(k == 0), stop=(k == KC - 1)),
                           reads=[W, sc], writes=[P])
                sch.op("dve", lambda E, P=P, Bm=Bm, R=R: E.tensor_tensor(out=R.ap[0:1, :], in0=P.ap[0:1, :], in1=Bm.ap[0:1, :], op=ALU.add),
                       reads=[P, Bm], writes=[R])
                sch.op("sp", lambda E, R=R, n=n, l=l: E.dma_start(out=T.mod_d[l:l + 1, n * 512:(n + 1) * 512], in_=R.ap[0:1, :]),
                       reads=[R], dma=True)
                it += 1

    def phase_u(l, src_d, m_shift, m_scale):
        new_phase()
        tmp = arena.alloc([128], F32, "pp_tmp")
        tmp2 = arena.alloc([128], F32, "pp_tmp2")
        sc1p = arena.alloc([KC], F32, "sc1p")
        sh = arena.alloc([KC], F32, "sh")
        load_pp(T.mod_d[l, m_scale * D:(m_scale + 1) * D], sc1p, tmp, PS[0], add_one=True)
        load_pp(T.mod_d[l, m_shift * D:(m_shift + 1) * D], sh, tmp2, PS[1], add_one=False)
        xb = [arena.alloc([D], F32, f"xb{i}") for i in range(2)]
        ub = [arena.alloc([KC, 512], BF16, f"ub{i}") for i in range(2)]
        pi = 0
        xload = {}

        def load_x(t_):
            if t_ < NT and t_ not in xload:
                X_ = xb[t_ % 2]
                sch.op("sp", lambda E, X_=X_, t_=t_: E.dma_start(out=X_.ap, in_=src_d[t_ * 128:(t_ + 1) * 128, :]), writes=[X_], dma=True)
                xload[t_] = X_

        for tt in range(NT):
            load_x(tt)
            X = xload[tt]
            U = ub[(tt // 4) % 2]
            j = tt % 4
            for k4 in range(KC // 4):
                P = PS[2 + pi % 6]; pi += 1
                for kk in range(4):
                    k = k4 * 4 + kk
                    sch.op("pe", lambda E, P=P, X=X, k=k, kk=kk: E.transpose(out=P.ap[:, kk * 128:(kk + 1) * 128],
                                                                            in_=X.ap[:, k * 128:(k + 1) * 128], identity=ident),
                           reads=[X, CST], writes=[P])
                for kk in range(4):
                    k = k4 * 4 + kk
                    sch.op("act", lambda E, P=P, U=U, k=k, kk=kk, j=j: E.activation(
                        out=U.ap[:, k, j * 128:(j + 1) * 128], in_=P.ap[:, kk * 128:(kk + 1) * 128], func=AF.Identity,
                        scale=sc1p.ap[:, k:k + 1], bias=sh.ap[:, k:k + 1]),
                        reads=[P, sc1p, sh], writes=[U.sub((k, j))])
            load_x(tt + 1)
            if j == 3:
                tb = tt // 4
                sch.op("sp", lambda E, U=U, tb=tb: E.dma_start(out=blk(T.uT_d, KC, tb), in_=U.ap), reads=[U], dma=True)

    def gemm_F(w_rows_ap, KCn, col_tiles, actT_d, G, evac):
        wv = w_rows_ap.rearrange("(k p) c -> p k c", p=128)
        wbuf = [arena.alloc([KCn, G * 128], BF16, f"gw{i}") for i in range(2)]
        abuf = [arena.alloc([KCn, 512], BF16, f"ga{i}") for i in range(2)]
        groups = [col_tiles[i:i + G] for i in range(0, len(col_tiles), G)]
        ai = 0
        pi = 0
        iters = [(gi, tb) for gi in range(len(groups)) for tb in range(NB512)]
        loaded = {}

        def load_a(it):
            if it < len(iters) and it not in loaded:
                A_ = abuf[it % 2]
                tb_ = iters[it][1]
                sch.op("sp", lambda E, A_=A_, tb_=tb_: E.dma_start(out=A_.ap, in_=blk(actT_d, KCn, tb_)), writes=[A_], dma=True)
                loaded[it] = A_

        for gi, grp in enumerate(groups):
            W = wbuf[gi % 2]
            runs = []
            for ti, (c0, cw) in enumerate(grp):
                if runs and runs[-1][1] + runs[-1][2] == c0 and runs[-1][2] % 128 == 0:
                    runs[-1][2] += cw; runs[-1][3].append(ti)
                else:
                    runs.append([ti, c0, cw, [ti]])
            for (ti0, c0, tw, tis) in runs:
                sch.op("pool", lambda E, W=W, ti0=ti0, c0=c0, tw=tw: E.dma_start(out=W.ap[:, :, ti0 * 128:ti0 * 128 + tw], in_=wv[:, :, c0:c0 + tw]),
                       writes=[W.sub(t_) for t_ in tis], dma=True)
            for tb in range(NB512):
                load_a(ai)
                A = loaded[ai]
                load_a(ai + 1)
                ai += 1
                for ti, (c0, cw) in enumerate(grp):
                    P = PS[pi % 8]; pi += 1
                    for k in range(KCn):
                        sch.op("pe", lambda E, P=P, W=W, A=A, k=k, ti=ti, cw=cw: E.matmul(
                            P.ap[0:cw, :], lhsT=W.ap[:, k, ti * 128:ti * 128 + cw], rhs=A.ap[:, k, :], start=(k == 0), stop=(k == KCn - 1)),
                            reads=[W.sub(ti), A], writes=[P])
                    evac(gi * G + ti, tb, P, cw)

    def gemm_T(w_rows_ap, KCn, col_tiles, actT_d, TB, evac):
        wv = w_rows_ap.rearrange("(k p) c -> p k c", p=128)
        W = arena.alloc([KCn, 512], BF16, "tw")
        abuf = [arena.alloc([KCn, TB], BF16, f"ta{i}") for i in range(2)]
        ai = 0
        pi = 0
        iters = [(ci, tb) for ci in range(len(col_tiles)) for tb in range(S // TB)]
        loaded = {}

        def load_a(it):
            if it < len(iters) and it not in loaded:
                A_ = abuf[it % 2]
                tb_ = iters[it][1]
                sch.op("sp", lambda E, A_=A_, tb_=tb_: E.dma_start(out=A_.ap, in_=blk(actT_d, KCn, tb_)), writes=[A_], dma=True)
                loaded[it] = A_

        for ci, (c0, cw) in enumerate(col_tiles):
            sch.op("pool", lambda E, c0=c0, cw=cw: E.dma_start(out=W.ap[:, :, 0:cw], in_=wv[:, :, c0:c0 + cw]), writes=[W], dma=True)
            for tb in range(S // TB):
                load_a(ai)
                A = loaded[ai]
                load_a(ai + 1)
                ai += 1
                for t in range(TB // 128):
                    P = PS[pi % 8]; pi += 1
                    for k in range(KCn):
                        sch.op("pe", lambda E, P=P, A=A, k=k, t=t, cw=cw: E.matmul(
                            P.ap[:, 0:cw], lhsT=A.ap[:, k, t * 128:(t + 1) * 128], rhs=W.ap[:, k, 0:cw], start=(k == 0), stop=(k == KCn - 1)),
                            reads=[W, A], writes=[P])
                    evac(ci, tb * (TB // 128) + t, P, c0, cw)

    def phase_inproj(l):
        new_phase()
        wl = T.w_in[l * D:(l + 1) * D, :]
        ob = [arena.alloc([4, 512], BF16, f"qko{i}") for i in range(2)]
        tiles = []
        for c0 in range(0, 2 * WA, 128):
            tiles.append((c0, 128, c0))
        for c0 in range(3 * WA, 3 * WA + WB + KVB, 128):
            cw = min(128, 3 * WA + WB + KVB - c0)
            tiles.append((c0, cw, 2 * WA + (c0 - 3 * WA)))
        st = {"n": 0}

        def evac(ti, tb, P, cw):
            r0 = tiles[ti][2]
            O = ob[(st["n"] // 4) % 2]; slot = st["n"] % 4; st["n"] += 1
            eng = "act" if st["n"] % 2 == 0 else "dve"
            if eng == "act":
                sch.op("act", lambda E: E.copy(out=O.ap[0:cw, slot, :], in_=P.ap[0:cw, :]), reads=[P], writes=[O.sub(slot)])
            else:
                sch.op("dve", lambda E: E.tensor_copy(out=O.ap[0:cw, slot, :], in_=P.ap[0:cw, :]), reads=[P], writes=[O.sub(slot)])
            sch.op("sp", lambda E: E.dma_start(out=T.qkT_d[r0:r0 + cw, tb * 512:(tb + 1) * 512], in_=O.ap[0:cw, slot, :]),
                   reads=[O.sub(slot)], dma=True)

        gemm_F(wl, KC, [(c0, cw) for (c0, cw, _) in tiles], T.uT_d, 4, evac)

    def phase_vproj(l):
        new_phase()
        wl = T.w_in[l * D:(l + 1) * D, :]
        ob = [arena.alloc([512], BF16, f"vo{i}") for i in range(4)]
        tiles = []
        dst = []
        for c0 in range(2 * WA, 3 * WA, 512):
            cw = min(512, 3 * WA - c0)
            tiles.append((c0, cw)); dst.append(c0 - 2 * WA)
        c0 = 3 * WA + WB + KVB
        tiles.append((c0, KVB)); dst.append(WA)
        st = {"n": 0}

        def evac(ci, tt, P, c0, cw):
            O = ob[st["n"] % 4]; st["n"] += 1
            d0 = dst[ci]
            if st["n"] % 2 == 0:
                sch.op("act", lambda E: E.copy(out=O.ap[:, 0:cw], in_=P.ap[:, 0:cw]), reads=[P], writes=[O])
            else:
                sch.op("dve", lambda E: E.tensor_copy(out=O.ap[:, 0:cw], in_=P.ap[:, 0:cw]), reads=[P], writes=[O])
            sch.op("sp", lambda E: E.dma_start(out=T.v_d[tt * 128:(tt + 1) * 128, d0:d0 + cw], in_=O.ap[:, 0:cw]), reads=[O], dma=True)

        gemm_T(wl, KC, tiles, T.uT_d, 512, evac)

    def phase_attn(l):
        new_phase()
        mixv = T.mixT_d.rearrange("(t p) (k s) -> p t k s", p=128, k=KC)
        q_sb = arena.alloc([S], BF16, "q_sb")
        k_sb = arena.alloc([S], BF16, "k_sb")
        NBmax = S // 128
        Vs = [arena.alloc([NBmax, 128], BF16, f"Vs{i}") for i in range(2)]
        Oacc = arena.alloc([S], F32, "Oacc")
        Dacc = arena.alloc([S], F32, "Dacc")
        mix_sb = arena.alloc([S], BF16, "mix_sb")
        Bt = [arena.alloc([256], F32, f"Bt{i}") for i in range(2)]
        tmpb = [arena.alloc([256], F32, f"tmp{i}") for i in range(3)]
        PT = [arena.alloc([256], BF16, f"PT{i}") for i in range(6)]
        esink = arena.alloc([1], F32, "esink")
        rden = [arena.alloc([128], F32, f"rden{i}") for i in range(2)]
        PS_S = [PS[0], PS[1], PS[2]]
        PS_O = [PS[3], PS[4]]
        PS_D = [PS[5], PS[6]]
        inv_a = 1.0 / math.sqrt(128.0)
        inv_b = 1.0 / math.sqrt(64.0)
        cnt = {"s": 0, "o": 0, "bt": 0, "tmp": 0, "vs": 0}

        for h in range(HA):
            slope = float(cfg.slopes[HB + h])
            sch.op("sp", lambda E, h=h: E.dma_start(out=q_sb.ap, in_=T.qkT_d[h * 128:(h + 1) * 128, :]), writes=[q_sb], dma=True)
            sch.op("sp", lambda E, h=h: E.dma_start(out=k_sb.ap, in_=T.qkT_d[WA + h * 128:WA + (h + 1) * 128, :]), writes=[k_sb], dma=True)
            for bi, d in enumerate((1, 4, 16)):
                cval = np.float32(slope) * np.float32(d)
                B = Bt[cnt["bt"] % 2]; cnt["bt"] += 1
                sch.op("dve", lambda E, B=B, cval=cval: E.scalar_tensor_tensor(out=B.ap, in0=DD, scalar=-float(cval), in1=MA,
                                                                               op0=ALU.mult, op1=ALU.add),
                       reads=[CST], writes=[B])
                L = S // d
                nb = L // 128
                for rho in range(d):
                    V = Vs[cnt["vs"] % 2]; cnt["vs"] += 1
                    vsrc = T.v_d[:, h * 128:(h + 1) * 128]
                    for j0 in range(0, nb, 16):
                        j1 = min(nb, j0 + 16)
                        src = vsrc[rho + j0 * 128 * d: rho + (j1 * 128 - 1) * d + 1: d, :].rearrange("(j p) e -> p j e", p=128)
                        sch.op("sp", lambda E, V=V, j0=j0, j1=j1, src=src: E.dma_start(out=V.ap[:, j0:j1, :], in_=src),
                               writes=[V.sub(j0)], dma=True)
                    qv = q_sb.ap[:, rho::d] if d > 1 else q_sb.ap
                    kv = k_sb.ap[:, rho::d] if d > 1 else k_sb.ap
                    Ov = Oacc.ap[:, rho::d] if d > 1 else Oacc.ap
                    Dv = Dacc.ap[:, rho::d] if d > 1 else Dacc.ap
                    prevPT = None
                    for j in range(nb):
                        nq = 256 if j + 1 < nb else 128
                        P = PS_S[cnt["s"] % 3]
                        TM = tmpb[cnt["s"] % 3]
                        Pt = PT[cnt["s"] % 6]; cnt["s"] += 1
                        sch.op("pe", lambda E, P=P, kv=kv, qv=qv, j=j, nq=nq: E.matmul(
                            P.ap[:, 0:nq], lhsT=kv[:, j * 128:(j + 1) * 128], rhs=qv[:, j * 128:j * 128 + nq], start=True, stop=True),
                            reads=[k_sb, q_sb], writes=[P])
                        sch.op("dve", lambda E, P=P, TM=TM, B=B, nq=nq: E.scalar_tensor_tensor(
                            out=TM.ap[:, 0:nq], in0=P.ap[:, 0:nq], scalar=inv_a, in1=B.ap[:, 0:nq], op0=ALU.mult, op1=ALU.add),
                            reads=[P, B], writes=[TM])
                        sch.op("act", lambda E, TM=TM, Pt=Pt, nq=nq: E.activation(out=Pt.ap[:, 0:nq], in_=TM.ap[:, 0:nq], func=AF.Exp),
                               reads=[TM], writes=[Pt])
                        PO = PS_O[cnt["o"] % 2]; PD = PS_D[cnt["o"] % 2]; cnt["o"] += 1
                        jsub = (j // 16) * 16
                        if j > 0:
                            psub = ((j - 1) // 16) * 16
                            pp = prevPT
                            sch.op("pe", lambda E, PO=PO, V=V, pp=pp, j=j: E.matmul(PO.ap[:, 0:128], lhsT=V.ap[:, j - 1, :], rhs=pp.ap[:, 128:256],
                                                                                  start=True, stop=False),
                                   reads=[V.sub(psub), pp], writes=[PO])
                            sch.op("pe", lambda E, PD=PD, pp=pp: E.matmul(PD.ap[:, 0:128], lhsT=ones_b, rhs=pp.ap[:, 128:256], start=True, stop=False),
                                   reads=[CSTB, pp], writes=[PD])
                        sch.op("pe", lambda E, PO=PO, V=V, Pt=Pt, j=j: E.matmul(PO.ap[:, 0:128], lhsT=V.ap[:, j, :], rhs=Pt.ap[:, 0:128],
                                                                              start=(j == 0), stop=True),
                               reads=[V.sub(jsub), Pt], writes=[PO])
                        sch.op("pe", lambda E, PD=PD, Pt=Pt, j=j: E.matmul(PD.ap[:, 0:128], lhsT=ones_b, rhs=Pt.ap[:, 0:128], start=(j == 0), stop=True),
                               reads=[CSTB, Pt], writes=[PD])
                        osl = Ov[:, j * 128:(j + 1) * 128]
                        dsl = Dv[:, j * 128:(j + 1) * 128]
                        if bi == 0:
                            sch.op("act", lambda E, PO=PO, osl=osl: E.copy(out=osl, in_=PO.ap[:, 0:128]), reads=[PO], writes=[Oacc.sub(0)])
                            sch.op("dve", lambda E, PD=PD, dsl=dsl: E.tensor_copy(out=dsl, in_=PD.ap[:, 0:128]), reads=[PD], writes=[Dacc.sub(0)])
                        else:
                            sch.op("dve", lambda E, PO=PO, osl=osl: E.tensor_tensor(out=osl, in0=PO.ap[:, 0:128], in1=osl, op=ALU.add),
                                   reads=[PO, Oacc.sub(0)], writes=[Oacc.sub(0)])
                            sch.op("dve", lambda E, PD=PD, dsl=dsl: E.tensor_tensor(out=dsl, in0=PD.ap[:, 0:128], in1=dsl, op=ALU.add),
                                   reads=[PD, Dacc.sub(0)], writes=[Dacc.sub(0)])
                        prevPT = Pt
            for s0 in range(0, S, 2048):
                s1 = min(S, s0 + 2048)
                sch.op("dve", lambda E, s0=s0, s1=s1: E.reciprocal(out=Dacc.ap[:, s0:s1], in_=Dacc.ap[:, s0:s1]), reads=[Dacc], writes=[Dacc])
                sch.op("pool", lambda E, s0=s0, s1=s1: E.tensor_tensor(out=mix_sb.ap[:, s0:s1], in0=Oacc.ap[:, s0:s1], in1=Dacc.ap[:, s0:s1], op=ALU.mult),
                       reads=[Oacc, Dacc], writes=[mix_sb])
            sch.op("sp", lambda E, h=h: E.dma_start(out=mixv[:, :, h, :], in_=mix_sb.ap.rearrange("p (t s) -> p t s", s=512)), reads=[mix_sb], dma=True)

        Vz = [Vs[0], Vs[1]]
        nb = S // 128
        cur_kv = -1
        for pr in range(HB // 2):
            kvh = (2 * pr) // 8
            if kvh != cur_kv:
                cur_kv = kvh
                sch.op("pool", lambda E: E.memset(Vz[0].ap, 0.0), writes=[Vz[0]])
                sch.op("pool", lambda E: E.memset(Vz[1].ap, 0.0), writes=[Vz[1]])
                vsrc = T.v_d[:, WA + kvh * 64:WA + (kvh + 1) * 64]
                for j0 in range(0, nb, 16):
                    j1 = min(nb, j0 + 16)
                    src = vsrc[j0 * 128:j1 * 128, :].rearrange("(j p) e -> p j e", p=128)
                    sch.op("sp", lambda E, j0=j0, j1=j1, src=src: E.dma_start(out=Vz[0].ap[:, j0:j1, 0:64], in_=src), writes=[Vz[0]], dma=True)
                    sch.op("sp", lambda E, j0=j0, j1=j1, src=src: E.dma_start(out=Vz[1].ap[:, j0:j1, 64:128], in_=src), writes=[Vz[1]], dma=True)
                kr = 2 * WA + WB + kvh * 64
                sch.op("sp", lambda E, kr=kr: E.dma_start(out=k_sb.ap[0:64, :], in_=T.qkT_d[kr:kr + 64, :]), writes=[k_sb], dma=True)
                sch.op("sp", lambda E, kr=kr: E.dma_start(out=k_sb.ap[64:128, :], in_=T.qkT_d[kr:kr + 64, :]), writes=[k_sb], dma=True)
            qr = 2 * WA + pr * 128
            sch.op("sp", lambda E, qr=qr: E.dma_start(out=q_sb.ap, in_=T.qkT_d[qr:qr + 128, :]), writes=[q_sb], dma=True)
            srow = l * (HB // 2) + pr
            sch.op("sp", lambda E, srow=srow: E.dma_start(out=esink.ap, in_=T.sinks[srow:srow + 1, :].rearrange("o p -> p o")), writes=[esink], dma=True)
            sch.op("act", lambda E: E.activation(out=esink.ap, in_=esink.ap, func=AF.Exp), reads=[esink], writes=[esink])
            Bh = []
            for hh in range(2):
                slope = float(cfg.slopes[2 * pr + hh])
                B = Bt[hh]
                sch.op("dve", lambda E, B=B, slope=slope: E.scalar_tensor_tensor(out=B.ap, in0=DD, scalar=-slope, in1=MB, op0=ALU.mult, op1=ALU.add),
                       reads=[CST], writes=[B])
                Bh.append(B)
            prev = [None, None]
            for j in range(nb):
                nq = 256 if j + 1 < nb else 128
                cur = []
                for hh in range(2):
                    P = PS_S[cnt["s"] % 3]
                    TM = tmpb[cnt["s"] % 3]
                    Pt = PT[cnt["s"] % 6]; cnt["s"] += 1
                    lo, hi = hh * 64, (hh + 1) * 64
                    sch.op("pe", lambda E, P=P, j=j, nq=nq, lo=lo, hi=hi: E.matmul(
                        P.ap[:, 0:nq], lhsT=k_sb.ap[lo:hi, j * 128:(j + 1) * 128], rhs=q_sb.ap[lo:hi, j * 128:j * 128 + nq], start=True, stop=True),
                        reads=[k_sb, q_sb], writes=[P])
                    sch.op("dve", lambda E, P=P, TM=TM, hh=hh, nq=nq: E.scalar_tensor_tensor(
                        out=TM.ap[:, 0:nq], in0=P.ap[:, 0:nq], scalar=inv_b, in1=Bh[hh].ap[:, 0:nq], op0=ALU.mult, op1=ALU.add),
                        reads=[P, Bh[hh]], writes=[TM])
                    sch.op("act", lambda E, TM=TM, Pt=Pt, nq=nq: E.activation(out=Pt.ap[:, 0:nq], in_=TM.ap[:, 0:nq], func=AF.Exp),
                           reads=[TM], writes=[Pt])
                    cur.append(Pt)
                PO = PS_O[cnt["o"] % 2]; PD = PS_D[cnt["o"] % 2]
                R = rden[cnt["o"] % 2]; cnt["o"] += 1
                mms = []
                if j > 0:
                    for hh in range(2):
                        mms.append((Vz[hh], j - 1, prev[hh], 128, 256, hh))
                for hh in range(2):
                    mms.append((Vz[hh], j, cur[hh], 0, 128, hh))
                for mi, (Vb, jj, Pb, a0, a1, hh) in enumerate(mms):
                    first = (mi == 0); last = (mi == len(mms) - 1)
                    sch.op("pe", lambda E, PO=PO, Vb=Vb, jj=jj, Pb=Pb, a0=a0, a1=a1, first=first, last=last: E.matmul(
                        PO.ap[:, 0:128], lhsT=Vb.ap[:, jj, :], rhs=Pb.ap[:, a0:a1], start=first, stop=last),
                        reads=[Vb, Pb], writes=[PO])
                for mi, (Vb, jj, Pb, a0, a1, hh) in enumerate(mms):
                    first = (mi == 0); last = (mi == len(mms) - 1)
                    sch.op("pe", lambda E, PD=PD, Pb=Pb, a0=a0, a1=a1, hh=hh, first=first, last=last: E.matmul(
                        PD.ap[:, 0:128], lhsT=onesz[hh], rhs=Pb.ap[:, a0:a1], start=first, stop=last),
                        reads=[CSTB, Pb], writes=[PD])
                sch.op("dve", lambda E, PD=PD, R=R: E.tensor_scalar(out=R.ap, in0=PD.ap[:, 0:128], scalar1=esink.ap[:, 0:1], scalar2=None, op0=ALU.add),
                       reads=[PD, esink], writes=[R])
                sch.op("dve", lambda E, R=R: E.reciprocal(out=R.ap, in_=R.ap), reads=[R], writes=[R])
                sch.op("dve", lambda E, PO=PO, R=R, j=j: E.tensor_tensor(out=mix_sb.ap[:, j * 128:(j + 1) * 128], in0=PO.ap[:, 0:128], in1=R.ap, op=ALU.mult),
                       reads=[PO, R], writes=[mix_sb.sub(0)])
                prev = cur
            mk = WA // 128 + pr
            sch.op("sp", lambda E, mk=mk: E.dma_start(out=mixv[:, :, mk, :], in_=mix_sb.ap.rearrange("p (t s) -> p t s", s=512)), reads=[mix_sb], dma=True)

    def phase_projT(w_rows_ap, KCn, actT_d, TB):
        new_phase()
        ob = [arena.alloc([512], F32, f"po{i}") for i in range(4)]
        st = {"n": 0}

        def evac(ci, tt, P, c0, cw):
            O = ob[st["n"] % 4]; st["n"] += 1
            if st["n"] % 2 == 0:
                sch.op("act", lambda E: E.copy(out=O.ap[:, 0:cw], in_=P.ap[:, 0:cw]), reads=[P], writes=[O])
            else:
                sch.op("dve", lambda E: E.tensor_copy(out=O.ap[:, 0:cw], in_=P.ap[:, 0:cw]), reads=[P], writes=[O])
            sch.op("sp", lambda E: E.dma_start(out=T.y_d[tt * 128:(tt + 1) * 128, c0:c0 + cw], in_=O.ap[:, 0:cw]), reads=[O], dma=True)

        gemm_T(w_rows_ap, KCn, [(c0, 512) for c0 in range(0, D, 512)], actT_d, TB, evac)

    def phase_ln(l, xres_d, m_gate, g_in, b_in, dst_d):
        new_phase()
        gate = arena.alloc([D], F32, "gate")
        gbc = arena.alloc([D], F32, "gbc")
        bbc = arena.alloc([D], F32, "bbc")
        sch.op("sp", lambda E: E.dma_start(out=gate.ap, in_=T.mod_d[l:l + 1, m_gate * D:(m_gate + 1) * D].partition_broadcast(128)), writes=[gate], dma=True)
        sch.op("sp", lambda E: E.dma_start(out=gbc.ap, in_=g_in[l:l + 1, :].partition_broadcast(128)), writes=[gbc], dma=True)
        sch.op("sp", lambda E: E.dma_start(out=bbc.ap, in_=b_in[l:l + 1, :].partition_broadcast(128)), writes=[bbc], dma=True)
        sch.op("pool", lambda E: E.tensor_scalar(out=gate.ap, in0=gate.ap, scalar1=1.0, scalar2=None, op0=ALU.add), reads=[gate], writes=[gate])
        xb = [arena.alloc([D], F32, f"lx{i}") for i in range(2)]
        yb = [arena.alloc([D], F32, f"ly{i}") for i in range(2)]
        zb = [arena.alloc([D], F32, f"lz{i}") for i in range(2)]
        stat = [arena.alloc([8], F32, f"st{i}") for i in range(2)]
        lnload = set()

        def load_xy(t_):
            if t_ < NT and t_ not in lnload:
                X_ = xb[t_ % 2]; Y_ = yb[t_ % 2]
                sch.op("sp", lambda E, X_=X_, t_=t_: E.dma_start(out=X_.ap, in_=xres_d[t_ * 128:(t_ + 1) * 128, :]), writes=[X_], dma=True)
                sch.op("sp", lambda E, Y_=Y_, t_=t_: E.dma_start(out=Y_.ap, in_=T.y_d[t_ * 128:(t_ + 1) * 128, :]), writes=[Y_], dma=True)
                lnload.add(t_)

        for tt in range(NT):
            X = xb[tt % 2]; Y = yb[tt % 2]; Z = zb[tt % 2]; St = stat[tt % 2]
            load_xy(tt)
            sch.op("pool", lambda E, Y=Y: E.tensor_tensor(out=Y.ap, in0=Y.ap, in1=gate.ap, op=ALU.mult), reads=[Y, gate], writes=[Y])
            sch.op("dve", lambda E, X=X, Y=Y, Z=Z, St=St: E.scalar_tensor_tensor(out=Z.ap, in0=X.ap, scalar=float(cfg.alpha), in1=Y.ap,
                                                                                 op0=ALU.mult, op1=ALU.add, accum_out=St.ap[:, 0:1]),
                   reads=[X, Y], writes=[Z, St])
            sch.op("dve", lambda E, St=St: E.tensor_scalar(out=St.ap[:, 1:2], in0=St.ap[:, 0:1], scalar1=1.0 / D, scalar2=None, op0=ALU.mult),
                   reads=[St], writes=[St])
            sch.op("dve", lambda E, Z=Z, St=St: E.tensor_scalar(out=Z.ap, in0=Z.ap, scalar1=St.ap[:, 1:2], scalar2=None, op0=ALU.subtract),
                   reads=[Z, St], writes=[Z])
            sch.op("act", lambda E, Z=Z, X=X, St=St: E.activation(out=X.ap, in_=Z.ap, func=AF.Square, accum_out=St.ap[:, 2:3]),
                   reads=[Z], writes=[X, St])
            sch.op("dve", lambda E, St=St: E.tensor_scalar(out=St.ap[:, 3:4], in0=St.ap[:, 2:3], scalar1=1.0 / D, scalar2=LN_EPS, op0=ALU.mult, op1=ALU.add),
                   reads=[St], writes=[St])
            sch.op("act", lambda E, St=St: E.activation(out=St.ap[:, 4:5], in_=St.ap[:, 3:4], func=AF.Sqrt), reads=[St], writes=[St])
            sch.op("dve", lambda E, St=St: E.reciprocal(out=St.ap[:, 5:6], in_=St.ap[:, 4:5]), reads=[St], writes=[St])
            sch.op("dve", lambda E, Z=Z, St=St: E.scalar_tensor_tensor(out=Z.ap, in0=Z.ap, scalar=St.ap[:, 5:6], in1=gbc.ap, op0=ALU.mult, op1=ALU.mult),
                   reads=[Z, St, gbc], writes=[Z])
            sch.op("pool", lambda E, Z=Z: E.tensor_tensor(out=Z.ap, in0=Z.ap, in1=bbc.ap, op=ALU.add), reads=[Z, bbc], writes=[Z])
            load_xy(tt + 1)
            sch.op("sp", lambda E, Z=Z, tt=tt: E.dma_start(out=dst_d[tt * 128:(tt + 1) * 128, :], in_=Z.ap), reads=[Z], dma=True)

    def phase_up(l):
        new_phase()
        wl = T.w_up[l * D:(l + 1) * D, :]
        CP = 2
        assert FC % CP == 0
        cwt = arena.alloc([3, 128], F32, "cwt")
        cwp = [arena.alloc([4], F32, f"cwp{i}") for i in range(2 * CP * 2)]
        carry = [arena.alloc([2], F32, f"carry{i}") for i in range(2 * CP)]
        acc = [arena.alloc([512], F32, f"acc{i}") for i in range(2 * CP * 2)]
        aout = [arena.alloc([CP, 512], BF16, f"aout{i}") for i in range(2)]
        rowt = [arena.alloc([128], F32, f"rowt{i}") for i in range(2)]
        wv = wl.rearrange("(k p) c -> p k c", p=128)
        aTv = T.aT_d.rearrange("(t p) (k s) -> p t k s", p=128, k=FC)
        wbuf = [arena.alloc([KC, 2 * CP * 128], BF16, f"uw{i}") for i in range(2)]
        abuf = [arena.alloc([KC, 512], BF16, f"ua{i}") for i in range(2)]
        ai = 0
        pi = 0
        ao = 0
        loaded = {}
        n_it = (FC // CP) * NB512

        def load_a(it):
            if it < n_it and it not in loaded:
                A_ = abuf[it % 2]
                tb_ = it % NB512
                sch.op("sp", lambda E, A_=A_, tb_=tb_: E.dma_start(out=A_.ap, in_=blk(T.uT_d, KC, tb_)), writes=[A_], dma=True)
                loaded[it] = A_

        for gi in range(FC // CP):
            W = wbuf[gi % 2]
            cols = []
            for i in range(CP):
                cols.append((gi * CP + i) * 128)
            for i in range(CP):
                cols.append(DFF + (gi * CP + i) * 128)
            for half in range(2):
                c0h = cols[half * CP]
                sch.op("pool", lambda E, W=W, half=half, c0h=c0h: E.dma_start(out=W.ap[:, :, half * CP * 128:(half + 1) * CP * 128],
                                                                           in_=wv[:, :, c0h:c0h + CP * 128]),
                       writes=[W.sub(half * CP + i_) for i_ in range(CP)], dma=True)
            for ti, c0 in enumerate(cols):
                RT = rowt[(gi * 2 * CP + ti) % 2]
                CW = cwp[(gi % 2) * 2 * CP + ti]
                sch.op("sp", lambda E, RT=RT, c0=c0: E.dma_start(out=RT.ap[0:3, :], in_=T.conv_w[l * 3:(l + 1) * 3, c0:c0 + 128]), writes=[RT], dma=True)
                sch.op("sp", lambda E, RT=RT, c0=c0: E.dma_start(out=RT.ap[3:4, :], in_=T.conv_b[l:l + 1, c0:c0 + 128]), writes=[RT], dma=True)
                PX = PS[pi % 8]; pi += 1
                sch.op("pe", lambda E, PX=PX, RT=RT: E.transpose(out=PX.ap[:, 0:4], in_=RT.ap[0:4, :], identity=ident[0:4, 0:4]),
                       reads=[RT, CST], writes=[PX])
                sch.op("dve", lambda E, PX=PX, CW=CW: E.tensor_copy(out=CW.ap, in_=PX.ap[:, 0:4]), reads=[PX], writes=[CW])
                sch.op("pool", lambda E, ti=ti: E.memset(carry[ti].ap, 0.0), writes=[carry[ti]])
            for tb in range(NB512):
                load_a(ai)
                A = loaded[ai]
                load_a(ai + 1)
                ai += 1
                accs = []
                for ti in range(2 * CP):
                    P = PS[pi % 8]; pi += 1
                    for k in range(KC):
                        sch.op("pe", lambda E, P=P, W=W, A=A, k=k, ti=ti: E.matmul(
                            P.ap[:, :], lhsT=W.ap[:, k, ti * 128:(ti + 1) * 128], rhs=A.ap[:, k, :], start=(k == 0), stop=(k == KC - 1)),
                            reads=[W.sub(ti), A], writes=[P])
                    CW = cwp[(gi % 2) * 2 * CP + ti]
                    AC = acc[(tb % 2) * 2 * CP + ti]
                    CA = carry[ti]
                    sch.op("act", lambda E, P=P, AC=AC, CW=CW: E.activation(out=AC.ap, in_=P.ap, func=AF.Identity, scale=CW.ap[:, 2:3], bias=CW.ap[:, 3:4]),
                           reads=[P, CW], writes=[AC])
                    sch.op("dve", lambda E, P=P, AC=AC, CW=CW: E.scalar_tensor_tensor(out=AC.ap[:, 1:512], in0=P.ap[:, 0:511], scalar=CW.ap[:, 1:2],
                                                                                      in1=AC.ap[:, 1:512], op0=ALU.mult, op1=ALU.add),
                           reads=[P, CW, AC], writes=[AC])
                    sch.op("dve", lambda E, P=P, AC=AC, CW=CW: E.scalar_tensor_tensor(out=AC.ap[:, 2:512], in0=P.ap[:, 0:510], scalar=CW.ap[:, 0:1],
                                                                                      in1=AC.ap[:, 2:512], op0=ALU.mult, op1=ALU.add),
                           reads=[P, CW, AC], writes=[AC])
                    sch.op("dve", lambda E, AC=AC, CW=CW, CA=CA: E.scalar_tensor_tensor(out=AC.ap[:, 0:1], in0=CA.ap[:, 1:2], scalar=CW.ap[:, 1:2],
                                                                                        in1=AC.ap[:, 0:1], op0=ALU.mult, op1=ALU.add),
                           reads=[CA, CW, AC], writes=[AC])
                    sch.op("dve", lambda E, AC=AC, CW=CW, CA=CA: E.scalar_tensor_tensor(out=AC.ap[:, 0:2], in0=CA.ap[:, 0:2], scalar=CW.ap[:, 0:1],
                                                                                        in1=AC.ap[:, 0:2], op0=ALU.mult, op1=ALU.add),
                           reads=[CA, CW, AC], writes=[AC])
                    sch.op("dve", lambda E, P=P, CA=CA: E.tensor_copy(out=CA.ap, in_=P.ap[:, 510:512]), reads=[P], writes=[CA])
                    accs.append(AC)
                AO = aout[ao % 2]; ao += 1
                for i in range(CP):
                    G_ = accs[i]; V_ = accs[CP + i]
                    sch.op("act", lambda E, G_=G_: E.activation(out=G_.ap, in_=G_.ap, func=AF.Silu), reads=[G_], writes=[G_])
                    sch.op("pool", lambda E, G_=G_, V_=V_, AO=AO, i=i: E.tensor_tensor(out=AO.ap[:, i, :], in0=G_.ap, in1=V_.ap, op=ALU.mult),
                           reads=[G_, V_], writes=[AO.sub(i)])
                r0 = gi * CP * 128
                k0 = gi * CP
                for i in range(CP):
                    dst = aTv[:, tb * 4:(tb + 1) * 4, k0 + i, :]
                    sch.op("sp", lambda E, AO=AO, dst=dst, i=i: E.dma_start(out=dst, in_=AO.ap[:, i, :].rearrange("p (j s) -> p j s", j=4)),
                           reads=[AO.sub(i)], dma=True)

    for (ph, l) in plan["phases"]:
        if ph == "mod":
            phase_mod()
        elif ph == "A":
            cur = T.x if (l == 0 or plan.get("layer_local")) else T.x2_d
            phase_u(l, cur, 0, 1)
            phase_inproj(l)
            phase_vproj(l)
            phase_attn(l)
            phase_projT(T.w_out[l * D:(l + 1) * D, :], KC, T.mixT_d, 512)
            phase_ln(l, cur, 2, T.ln1_g, T.ln1_b, T.x1_d)
        elif ph == "B":
            phase_u(l, T.x1_d, 3, 4)
            phase_up(l)
            phase_projT(T.w_down[l * DFF:(l + 1) * DFF, :], FC, T.aT_d, 128)
            phase_ln(l, T.x1_d, 5, T.ln2_g, T.ln2_b, T.x2_d)
    sch.fence()
    with nc.Block() as block:
        sch.emit(block)
    return nc, used_inputs


def make_consts():
    k = np.arange(128, dtype=np.float32)[:, None]
    q = np.arange(128, dtype=np.float32)[None, :]
    ddiag = q - k
    dprev = 128.0 + q - k
    DD = np.concatenate([ddiag, dprev], axis=1)
    madiag = np.where(ddiag >= 0, 0.0, NEG)
    MA = np.concatenate([madiag, np.where(dprev <= 128, 0.0, NEG)], axis=1)
    MB = np.concatenate([madiag, np.where(dprev <= 127, 0.0, NEG)], axis=1)
    DDm = DD.copy()
    ident = np.eye(128, dtype=np.float32)
    return np.ascontiguousarray(np.concatenate([ident, DDm, MA, MB], axis=1).astype(np.float32))


def _f(a):
    return np.ascontiguousarray(np.asarray(a, dtype=np.float32))


def host_arrays(cfg, inputs):
    S, D, DEPTH = cfg.S, cfg.D, cfg.DEPTH
    sk = _f(inputs["sinks"]).reshape(DEPTH, cfg.HB // 2, 2)
    sk = np.ascontiguousarray(np.repeat(sk, 64, axis=2))
    return {
        "x": _f(inputs["x"]).reshape(S, D),
        "c": _f(inputs["c"]).reshape(cfg.KC, 128),
        "w_mod": _f(inputs["w_mod"]), "b_mod": _f(inputs["b_mod"]),
        "w_in": _f(inputs["w_in"]), "sinks": sk, "w_out": _f(inputs["w_out"]),
        "ln1_g": _f(inputs["ln1_g"]), "ln1_b": _f(inputs["ln1_b"]),
        "w_up": _f(inputs["w_up"]), "conv_w": _f(inputs["conv_w"]), "conv_b": _f(inputs["conv_b"]),
        "w_down": _f(inputs["w_down"]), "ln2_g": _f(inputs["ln2_g"]), "ln2_b": _f(inputs["ln2_b"]),
    }


def layer_slice(name, arr, l0, l1):
    a = arr[l0:l1]
    if a.ndim == 3:
        return np.ascontiguousarray(a.reshape(a.shape[0] * a.shape[1], a.shape[2]))
    return np.ascontiguousarray(a)


_PROG_CACHE = {}


def launch(cfg, plan, host, l0, l1, extra, debug=False, trace=False):
    key = (cfg.S, cfg.D, tuple(plan["phases"]), plan.get("depth", 1), debug)
    if key not in _PROG_CACHE:
        _PROG_CACHE[key] = build_program(cfg, plan, debug=debug)
    nc, used = _PROG_CACHE[key]
    im = {}
    for n in used:
        if n in extra:
            im[n] = extra[n]
        elif n == "consts":
            im[n] = make_consts()
        elif n in ("x", "c"):
            im[n] = host[n]
        else:
            im[n] = layer_slice(n, host[n], l0, l1)
    res = run_bass_kernel_spmd(nc, [im], core_ids=[0], trace=trace)
    return res.results[0]


def run_fused(cfg, inputs, debug=False):
    host = host_arrays(cfg, inputs)
    phases = [("mod", 0)]
    for l in range(cfg.DEPTH):
        phases += [("A", l), ("B", l)]
    plan = dict(phases=phases, depth=cfg.DEPTH, ext_out={"x2_d"})
    r = launch(cfg, plan, host, 0, cfg.DEPTH, {}, debug=debug)
    return r


def run_multi(cfg, inputs):
    host = host_arrays(cfg, inputs)
    r = launch(cfg, dict(phases=[("mod", 0)], depth=cfg.DEPTH, ext_out={"mod_d"}), host, 0, cfg.DEPTH, {})
    mod = np.asarray(r["mod_d"], dtype=np.float32)
    x = host["x"]
    planA = dict(phases=[("A", 0)], depth=1, layer_local=True, ext_in={"mod_d"}, ext_out={"x1_d"})
    planB = dict(phases=[("B", 0)], depth=1, layer_local=True, ext_in={"mod_d", "x1_d"}, ext_out={"x2_d"})
    for l in range(cfg.DEPTH):
        ml = np.ascontiguousarray(mod[l:l + 1])
        r = launch(cfg, planA, host, l, l + 1, {"mod_d": ml, "x": x})
        x1 = np.ascontiguousarray(np.asarray(r["x1_d"], dtype=np.float32))
        r = launch(cfg, planB, host, l, l + 1, {"mod_d": ml, "x1_d": x1})
        x = np.ascontiguousarray(np.asarray(r["x2_d"], dtype=np.float32))
    return x


def kernel(**inputs):
    cfg = Cfg()
    o = run_multi(cfg, inputs)
    return np.asarray(o, dtype=np.float32).reshape(1, cfg.S, cfg.D)
```

```python
import math
import numpy as np
import concourse.bass as bass
import concourse.mybir as mybir
from concourse.bass_utils import run_bass_kernel_spmd

F32 = mybir.dt.float32
BF16 = mybir.dt.bfloat16
AF = mybir.ActivationFunctionType
ALU = mybir.AluOpType
NEG = -30000.0
LN_EPS = 1e-5


class Cfg:
    def __init__(self, S=8192, D=4096, DEPTH=2):
        self.S, self.D, self.DEPTH = S, D, DEPTH
        self.KC = D // 128
        self.HA = (D // 2) // 128
        self.HB = (D // 2) // 64
        self.HKV = self.HB // 8
        self.WA = self.HA * 128
        self.WB = self.HB * 64
        self.KVB = self.HKV * 64
        self.INW = 3 * self.WA + self.WB + 2 * self.KVB
        self.DFF = 256 * math.ceil(8 * D / 3 / 256)
        self.FC = self.DFF // 128
        self.QKW = 2 * self.WA + self.WB + self.KVB
        self.VW = self.WA + self.KVB
        self.alpha = (2 * DEPTH) ** 0.25
        n_al = self.HA + self.HB
        i = np.arange(1, n_al + 1, dtype=np.float32)
        self.slopes = np.exp2(np.float32(-8.0) * i / np.float32(n_al)).astype(np.float32)


class Buf:
    def __init__(self, ap, name=""):
        self.ap = ap
        self.name = name
        self.subs = {}

    def sub(self, i):
        if i not in self.subs:
            self.subs[i] = ("sub", id(self), i)
        return (self, i)


class Sched:
    ENG = ["pe", "act", "dve", "pool", "sp"]

    def __init__(self, nc, ndma=6):
        self.nc = nc
        self.ops = {e: [] for e in self.ENG}
        self.needed = {e: set() for e in self.ENG}
        self.csem = {e: nc.alloc_semaphore(name=f"c_{e}") for e in self.ENG}
        self.dsem = {e: [nc.alloc_semaphore(name=f"d_{e}{i}") for i in range(ndma)] for e in ("sp", "pool", "act")}
        self.dcnt = {(e, i): 0 for e in ("sp", "pool", "act") for i in range(ndma)}
        self.drr = {e: 0 for e in ("sp", "pool", "act")}
        self.ndma = ndma
        self.last_w = {}
        self.readers = {}
        self.fence_tokens = []

    def _keys(self, k):
        if isinstance(k, tuple):
            b, i = k
            return [b.subs[i]], [("whole", id(b))]
        ks = [("whole", id(k))] + list(k.subs.values())
        return ks, []

    @staticmethod
    def _stream(tok):
        return (tok[0], tok[1]) if tok[0] == "c" else (tok[0], tok[1], tok[2])

    def _mark(self, tok):
        if tok[0] == "c":
            self.needed[tok[1]].add(tok[2])

    def op(self, eng, fn, reads=(), writes=(), dma=False):
        deps = {}

        def add(tok):
            if tok[0] == "c" and tok[1] == "pe" and eng == "pe" and not dma:
                return
            s = self._stream(tok)
            if s not in deps or deps[s][-1] < tok[-1]:
                deps[s] = tok

        for t in self.fence_tokens:
            add(t)
        for k in reads:
            own, chk = self._keys(k)
            for kk in own + chk:
                if kk in self.last_w:
                    add(self.last_w[kk])
        for k in writes:
            own, chk = self._keys(k)
            for kk in own + chk:
                if kk in self.last_w:
                    add(self.last_w[kk])
                for t in self.readers.get(kk, {}).values():
                    add(t)
        if dma:
            slot = self.drr[eng]
            self.drr[eng] = (slot + 1) % self.ndma
            prev = self.dcnt[(eng, slot)]
            if prev > 0:
                add(("d", eng, slot, prev))
            val = prev + 16
            self.dcnt[(eng, slot)] = val
            tok = ("d", eng, slot, val)
        else:
            tok = ("c", eng, len(self.ops[eng]))
        dl = list(deps.values())
        for t in dl:
            self._mark(t)
        self.ops[eng].append((fn, dl, tok))
        for k in writes:
            own, _ = self._keys(k)
            for kk in own:
                self.last_w[kk] = tok
                self.readers[kk] = {}
        for k in reads:
            own, _ = self._keys(k)
            for kk in own:
                self.readers.setdefault(kk, {})[self._stream(tok)] = tok
        return tok

    def fence(self):
        toks = []
        for e in self.ENG:
            for idx in range(len(self.ops[e]) - 1, -1, -1):
                if self.ops[e][idx][2][0] == "c":
                    t = ("c", e, idx)
                    toks.append(t)
                    self._mark(t)
                    break
        for (e, i), v in self.dcnt.items():
            if v > 0:
                toks.append(("d", e, i, v))
        self.fence_tokens = toks
        self.last_w = {}
        self.readers = {}

    def emit(self, block):
        cval = {}
        for e in self.ENG:
            m = {}
            c = 0
            for idx, (fn, dl, tok) in enumerate(self.ops[e]):
                if tok[0] == "c" and idx in self.needed[e]:
                    c += 1
                    m[idx] = c
            cval[e] = m
        final_dma = dict(self.dcnt)
        sched = self

        def run(e, E):
            waited = {}

            def wait(sem, key, val):
                if waited.get(key, 0) < val:
                    E.wait_ge(sem, val)
                    waited[key] = val

            for idx, (fn, dl, tok) in enumerate(sched.ops[e]):
                for d in dl:
                    if d[0] == "c":
                        wait(sched.csem[d[1]], ("c", d[1]), cval[d[1]][d[2]])
                    else:
                        wait(sched.dsem[d[1]][d[2]], ("d", d[1], d[2]), d[3])
                ins = fn(E)
                if tok[0] == "d":
                    ins.then_inc(sched.dsem[tok[1]][tok[2]], 16)
                elif idx in sched.needed[e]:
                    ins.then_inc(sched.csem[e], 1)
            if e == "sp":
                for (de, i), v in final_dma.items():
                    if v > 0:
                        wait(sched.dsem[de][i], ("d", de, i), v)

        @block.sync
        def _(E):
            run("sp", E)

        @block.tensor
        def _(E):
            run("pe", E)

        @block.scalar
        def _(E):
            run("act", E)

        @block.vector
        def _(E):
            run("dve", E)

        @block.gpsimd
        def _(E):
            run("pool", E)


class Arena:
    def __init__(self, nc, nbytes):
        self.t = nc.alloc_sbuf_tensor("arena", [128, nbytes // 4], F32)
        self.nbytes = nbytes
        self.off = 0

    def reset(self):
        self.off = 0

    def alloc(self, free_shape, dtype, name=""):
        esz = 2 if dtype == BF16 else 4
        n = int(np.prod(free_shape))
        nb = (n * esz + 31) // 32 * 32
        assert self.off + nb <= self.nbytes, f"arena overflow {name} {self.off}+{nb}>{self.nbytes}"
        w0 = self.off // 4
        ap = self.t[:, w0:w0 + nb // 4]
        self.off += nb
        if dtype != F32:
            ap = ap.bitcast(dtype)
        ap = ap[:, 0:n]
        if len(free_shape) == 2:
            ap = ap.rearrange("p (a b) -> p a b", a=free_shape[0], b=free_shape[1])
        elif len(free_shape) == 3:
            ap = ap.rearrange("p (a b c) -> p a b c", a=free_shape[0], b=free_shape[1], c=free_shape[2])
        return Buf(ap, name)


def build_program(cfg, plan, debug=False):
    S, D, KC = cfg.S, cfg.D, cfg.KC
    DEPTH = plan.get("depth", 1)
    HA, HB, HKV, WA, WB, KVB = cfg.HA, cfg.HB, cfg.HKV, cfg.WA, cfg.WB, cfg.KVB
    DFF, FC, QKW, VW = cfg.DFF, cfg.FC, cfg.QKW, cfg.VW
    NT = S // 128
    NB512 = S // 512
    nc = bass.Bass("TRN2", target_bir_lowering=False)
    used_inputs = []

    in_specs = {
        "x": [S, D], "c": [KC, 128], "w_mod": [DEPTH * D, 6 * D], "b_mod": [DEPTH, 6 * D],
        "w_in": [DEPTH * D, cfg.INW], "sinks": [DEPTH * HB // 2, 128], "w_out": [DEPTH * D, D],
        "ln1_g": [DEPTH, D], "ln1_b": [DEPTH, D], "w_up": [DEPTH * D, 2 * DFF],
        "conv_w": [DEPTH * 3, 2 * DFF], "conv_b": [DEPTH, 2 * DFF], "w_down": [DEPTH * DFF, D],
        "ln2_g": [DEPTH, D], "ln2_b": [DEPTH, D], "consts": [128, 128 + 3 * 256],
    }
    sc_specs = {
        "mod_d": ([DEPTH, 6 * D], F32), "uT_d": ([NB512 * 128, KC * 512], BF16), "qkT_d": ([QKW, S], BF16), "v_d": ([S, VW], BF16),
        "mixT_d": ([NB512 * 128, KC * 512], BF16), "y_d": ([S, D], F32), "x1_d": ([S, D], F32), "x2_d": ([S, D], F32),
        "aT_d": ([NT * 128, FC * 128], BF16),
    }
    cache = {}

    class _T:
        def __getattr__(self, name):
            if name in cache:
                return cache[name]
            if name in in_specs:
                ap = nc.dram_tensor(name, list(in_specs[name]), F32, kind="ExternalInput").ap()
                used_inputs.append(name)
            else:
                shape, dt = sc_specs[name]
                if name in plan.get("ext_in", ()):
                    ap = nc.dram_tensor(name, list(shape), dt, kind="ExternalInput").ap()
                    used_inputs.append(name)
                elif name in plan.get("ext_out", ()) or debug:
                    ap = nc.dram_tensor(name, list(shape), dt, kind="ExternalOutput").ap()
                else:
                    ap = nc.dram_tensor(name, list(shape), dt).ap()
            cache[name] = ap
            return ap

    T = _T()

    def blk(act_d, KCn, tb):
        return act_d[tb * 128:(tb + 1) * 128, :].rearrange("p (k s) -> p k s", k=KCn)

    sch = Sched(nc)
    arena = Arena(nc, 176 * 1024)
    cst_t = nc.alloc_sbuf_tensor("cst", [128, 128 + 3 * 256], F32)
    cstb_t = nc.alloc_sbuf_tensor("cstb", [128, 3 * 128], BF16)
    CST = Buf(cst_t[:], "cst")
    CSTB = Buf(cstb_t[:], "cstb")
    ident = cst_t[:, 0:128]
    DD = cst_t[:, 128:384]
    MA = cst_t[:, 384:640]
    MB = cst_t[:, 640:896]
    ones_b = cstb_t[:, 0:128]
    onesz = [cstb_t[:, 128:256], cstb_t[:, 256:384]]

    psum_ctx = [nc.psum_tensor(f"ps{i}", [128, 512], F32) for i in range(8)]
    ps_t = [c.__enter__() for c in psum_ctx]
    PS = [Buf(t[:], f"ps{i}") for i, t in enumerate(ps_t)]

    sch.op("sp", lambda E: E.dma_start(out=cst_t[:], in_=T.consts), writes=[CST], dma=True)
    sch.op("pool", lambda E: E.memset(cstb_t[:, 0:128], 1.0), writes=[CSTB])
    sch.op("pool", lambda E: E.memset(cstb_t[:, 128:384], 0.0), writes=[CSTB])
    sch.op("pool", lambda E: E.memset(cstb_t[:, 128:192], 1.0), writes=[CSTB])
    sch.op("pool", lambda E: E.memset(cstb_t[:, 320:384], 1.0), writes=[CSTB])

    def new_phase():
        sch.fence()
        arena.reset()

    def load_pp(src_row_ap, dst, tmp, psb, add_one=False):
        sch.op("sp", lambda E: E.dma_start(out=tmp.ap[0:KC, 0:128], in_=src_row_ap.rearrange("(k p) -> k p", p=128)),
               writes=[tmp], dma=True)
        sch.op("pe", lambda E: E.transpose(out=psb.ap[:, 0:KC], in_=tmp.ap[0:KC, 0:128], identity=ident[0:KC, 0:KC]),
               reads=[tmp, CST], writes=[psb])
        if add_one:
            sch.op("dve", lambda E: E.tensor_scalar(out=dst.ap, in0=psb.ap[:, 0:KC], scalar1=1.0, scalar2=None, op0=ALU.add),
                   reads=[psb], writes=[dst])
        else:
            sch.op("dve", lambda E: E.tensor_copy(out=dst.ap, in_=psb.ap[:, 0:KC]), reads=[psb], writes=[dst])

    def phase_mod():
        new_phase()
        crow = arena.alloc([128], F32, "crow")
        srow = arena.alloc([128], F32, "srow")
        sc = arena.alloc([KC], F32, "sc")
        sch.op("sp", lambda E: E.dma_start(out=crow.ap[0:KC, :], in_=T.c), writes=[crow], dma=True)
        sch.op("act", lambda E: E.activation(out=srow.ap[0:KC, :], in_=crow.ap[0:KC, :], func=AF.Silu),
               reads=[crow], writes=[srow])
        sch.op("pe", lambda E: E.transpose(out=PS[0].ap[:, 0:KC], in_=srow.ap[0:KC, :], identity=ident[0:KC, 0:KC]),
               reads=[srow, CST], writes=[PS[0]])
        sch.op("dve", lambda E: E.tensor_copy(out=sc.ap, in_=PS[0].ap[:, 0:KC]), reads=[PS[0]], writes=[sc])
        wb = [arena.alloc([KC, 512], F32, f"wmod{i}") for i in range(2)]
        bb = [arena.alloc([512], F32, f"bm{i}") for i in range(2)]
        rb = [arena.alloc([512], F32, f"rm{i}") for i in range(2)]
        it = 0
        NCT = 6 * D // 512
        mloaded = set()

        def load_w(it_):
            if it_ < DEPTH * NCT and it_ not in mloaded:
                l_, n_ = it_ // NCT, it_ % NCT
                W_ = wb[it_ % 2]; Bm_ = bb[it_ % 2]
                wv_ = T.w_mod[l_ * D:(l_ + 1) * D, :].rearrange("(k p) c -> p k c", p=128)
                sch.op("sp", lambda E, W_=W_, n_=n_, wv_=wv_: E.dma_start(out=W_.ap, in_=wv_[:, :, n_ * 512:(n_ + 1) * 512]),
                       writes=[W_], dma=True)
                sch.op("sp", lambda E, Bm_=Bm_, n_=n_, l_=l_: E.dma_start(out=Bm_.ap[0:1, :], in_=T.b_mod[l_:l_ + 1, n_ * 512:(n_ + 1) * 512]),
                       writes=[Bm_], dma=True)
                mloaded.add(it_)

        for l in range(DEPTH):
            for n in range(NCT):
                W = wb[it % 2]; Bm = bb[it % 2]; R = rb[it % 2]; P = PS[1 + it % 2]
                load_w(it)
                load_w(it + 1)
                for k in range(KC):
                    sch.op("pe", lambda E, W=W, P=P, k=k: E.matmul(P.ap[0:1, :], lhsT=sc.ap[:, k:k + 1], rhs=W.ap[:, k, :],
                                                                  start=(k == 0), stop=(k == KC - 1)),
                           reads=[W, sc], writes=[P])
                sch.op("dve", lambda E, P=P, Bm=Bm, R=R: E.tensor_tensor(out=R.ap[0:1, :], in0=P.ap[0:1, :], in1=Bm.ap[0:1, :], op=ALU.add),
                       reads=[P, Bm], writes=[R])
                sch.op("sp", lambda E, R=R, n=n, l=l: E.dma_start(out=T.mod_d[l:l + 1, n * 512:(n + 1) * 512], in_=R.ap[0:1, :]),
                       reads=[R], dma=True)
                it += 1

    def phase_u(l, src_d, m_shift, m_scale):
        new_phase()
        tmp = arena.alloc([128], F32, "pp_tmp")
        tmp2 = arena.alloc([128], F32, "pp_tmp2")
        sc1p = arena.alloc([KC], F32, "sc1p")
        sh = arena.alloc([KC], F32, "sh")
        load_pp(T.mod_d[l, m_scale * D:(m_scale + 1) * D], sc1p, tmp, PS[0], add_one=True)
        load_pp(T.mod_d[l, m_shift * D:(m_shift + 1) * D], sh, tmp2, PS[1], add_one=False)
        xb = [arena.alloc([D], F32, f"xb{i}") for i in range(2)]
        ub = [arena.alloc([KC, 512], BF16, f"ub{i}") for i in range(2)]
        pi = 0
        xload = {}

        def load_x(t_):
            if t_ < NT and t_ not in xload:
                X_ = xb[t_ % 2]
                sch.op("sp", lambda E, X_=X_, t_=t_: E.dma_start(out=X_.ap, in_=src_d[t_ * 128:(t_ + 1) * 128, :]), writes=[X_], dma=True)
                xload[t_] = X_

        for tt in range(NT):
            load_x(tt)
            X = xload[tt]
            U = ub[(tt // 4) % 2]
            j = tt % 4
            for k4 in range(KC // 4):
                P = PS[2 + pi % 6]; pi += 1
                for kk in range(4):
                    k = k4 * 4 + kk
                    sch.op("pe", lambda E, P=P, X=X, k=k, kk=kk: E.transpose(out=P.ap[:, kk * 128:(kk + 1) * 128],
                                                                            in_=X.ap[:, k * 128:(k + 1) * 128], identity=ident),
                           reads=[X, CST], writes=[P])
                for kk in range(4):
                    k = k4 * 4 + kk
                    sch.op("act", lambda E, P=P, U=U, k=k, kk=kk, j=j: E.activation(
                        out=U.ap[:, k, j * 128:(j + 1) * 128], in_=P.ap[:, kk * 128:(kk + 1) * 128], func=AF.Identity,
                        scale=sc1p.ap[:, k:k + 1], bias=sh.ap[:, k:k + 1]),
                        reads=[P, sc1p, sh], writes=[U.sub((k, j))])
            load_x(tt + 1)
            if j == 3:
                tb = tt // 4
                sch.op("sp", lambda E, U=U, tb=tb: E.dma_start(out=blk(T.uT_d, KC, tb), in_=U.ap), reads=[U], dma=True)

    def gemm_F(w_rows_ap, KCn, col_tiles, actT_d, G, evac):
        wv = w_rows_ap.rearrange("(k p) c -> p k c", p=128)
        wbuf = [arena.alloc([KCn, G * 128], BF16, f"gw{i}") for i in range(2)]
        abuf = [arena.alloc([KCn, 512], BF16, f"ga{i}") for i in range(2)]
        groups = [col_tiles[i:i + G] for i in range(0, len(col_tiles), G)]
        ai = 0
        pi = 0
        iters = [(gi, tb) for gi in range(len(groups)) for tb in range(NB512)]
        loaded = {}

        def load_a(it):
            if it < len(iters) and it not in loaded:
                A_ = abuf[it % 2]
                tb_ = iters[it][1]
                sch.op("sp", lambda E, A_=A_, tb_=tb_: E.dma_start(out=A_.ap, in_=blk(actT_d, KCn, tb_)), writes=[A_], dma=True)
                loaded[it] = A_

        for gi, grp in enumerate(groups):
            W = wbuf[gi % 2]
            runs = []
            for ti, (c0, cw) in enumerate(grp):
                if runs and runs[-1][1] + runs[-1][2] == c0 and runs[-1][2] % 128 == 0:
                    runs[-1][2] += cw; runs[-1][3].append(ti)
                else:
                    runs.append([ti, c0, cw, [ti]])
            for (ti0, c0, tw, tis) in runs:
                sch.op("pool", lambda E, W=W, ti0=ti0, c0=c0, tw=tw: E.dma_start(out=W.ap[:, :, ti0 * 128:ti0 * 128 + tw], in_=wv[:, :, c0:c0 + tw]),
                       writes=[W.sub(t_) for t_ in tis], dma=True)
            for tb in range(NB512):
                load_a(ai)
                A = loaded[ai]
                load_a(ai + 1)
                ai += 1
                for ti, (c0, cw) in enumerate(grp):
                    P = PS[pi % 8]; pi += 1
                    for k in range(KCn):
                        sch.op("pe", lambda E, P=P, W=W, A=A, k=k, ti=ti, cw=cw: E.matmul(
                            P.ap[0:cw, :], lhsT=W.ap[:, k, ti * 128:ti * 128 + cw], rhs=A.ap[:, k, :], start=(k == 0), stop=(k == KCn - 1)),
                            reads=[W.sub(ti), A], writes=[P])
                    evac(gi * G + ti, tb, P, cw)

    def gemm_T(w_rows_ap, KCn, col_tiles, actT_d, TB, evac):
        wv = w_rows_ap.rearrange("(k p) c -> p k c", p=128)
        W = arena.alloc([KCn, 512], BF16, "tw")
        abuf = [arena.alloc([KCn, TB], BF16, f"ta{i}") for i in range(2)]
        ai = 0
        pi = 0
        iters = [(ci, tb) for ci in range(len(col_tiles)) for tb in range(S // TB)]
        loaded = {}

        def load_a(it):
            if it < len(iters) and it not in loaded:
                A_ = abuf[it % 2]
                tb_ = iters[it][1]
                sch.op("sp", lambda E, A_=A_, tb_=tb_: E.dma_start(out=A_.ap, in_=blk(actT_d, KCn, tb_)), writes=[A_], dma=True)
                loaded[it] = A_

        for ci, (c0, cw) in enumerate(col_tiles):
            sch.op("pool", lambda E, c0=c0, cw=cw: E.dma_start(out=W.ap[:, :, 0:cw], in_=wv[:, :, c0:c0 + cw]), writes=[W], dma=True)
            for tb in range(S // TB):
                load_a(ai)
                A = loaded[ai]
                load_a(ai + 1)
                ai += 1
                for t in range(TB // 128):
                    P = PS[pi % 8]; pi += 1
                    for k in range(KCn):
                        sch.op("pe", lambda E, P=P, A=A, k=k, t=t, cw=cw: E.matmul(
                            P.ap[:, 0:cw], lhsT=A.ap[:, k, t * 128:(t + 1) * 128], rhs=W.ap[:, k, 0:cw], start=(k == 0), stop=(k == KCn - 1)),
                            reads=[W, A], writes=[P])
                    evac(ci, tb * (TB // 128) + t, P, c0, cw)

    def phase_inproj(l):
        new_phase()
        wl = T.w_in[l * D:(l + 1) * D, :]
        ob = [arena.alloc([4, 512], BF16, f"qko{i}") for i in range(2)]
        tiles = []
        for c0 in range(0, 2 * WA, 128):
            tiles.append((c0, 128, c0))
        for c0 in range(3 * WA, 3 * WA + WB + KVB, 128):
            cw = min(128, 3 * WA + WB + KVB - c0)
            tiles.append((c0, cw, 2 * WA + (c0 - 3 * WA)))
        st = {"n": 0}

        def evac(ti, tb, P, cw):
            r0 = tiles[ti][2]
            O = ob[(st["n"] // 4) % 2]; slot = st["n"] % 4; st["n"] += 1
            eng = "act" if st["n"] % 2 == 0 else "dve"
            if eng == "act":
                sch.op("act", lambda E: E.copy(out=O.ap[0:cw, slot, :], in_=P.ap[0:cw, :]), reads=[P], writes=[O.sub(slot)])
            else:
                sch.op("dve", lambda E: E.tensor_copy(out=O.ap[0:cw, slot, :], in_=P.ap[0:cw, :]), reads=[P], writes=[O.sub(slot)])
            sch.op("sp", lambda E: E.dma_start(out=T.qkT_d[r0:r0 + cw, tb * 512:(tb + 1) * 512], in_=O.ap[0:cw, slot, :]),
                   reads=[O.sub(slot)], dma=True)

        gemm_F(wl, KC, [(c0, cw) for (c0, cw, _) in tiles], T.uT_d, 4, evac)

    def phase_vproj(l):
        new_phase()
        wl = T.w_in[l * D:(l + 1) * D, :]
        ob = [arena.alloc([512], BF16, f"vo{i}") for i in range(4)]
        tiles = []
        dst = []
        for c0 in range(2 * WA, 3 * WA, 512):
            cw = min(512, 3 * WA - c0)
            tiles.append((c0, cw)); dst.append(c0 - 2 * WA)
        c0 = 3 * WA + WB + KVB
        tiles.append((c0, KVB)); dst.append(WA)
        st = {"n": 0}

        def evac(ci, tt, P, c0, cw):
            O = ob[st["n"] % 4]; st["n"] += 1
            d0 = dst[ci]
            if st["n"] % 2 == 0:
                sch.op("act", lambda E: E.copy(out=O.ap[:, 0:cw], in_=P.ap[:, 0:cw]), reads=[P], writes=[O])
            else:
                sch.op("dve", lambda E: E.tensor_copy(out=O.ap[:, 0:cw], in_=P.ap[:, 0:cw]), reads=[P], writes=[O])
            sch.op("sp", lambda E: E.dma_start(out=T.v_d[tt * 128:(tt + 1) * 128, d0:d0 + cw], in_=O.ap[:, 0:cw]), reads=[O], dma=True)

        gemm_T(wl, KC, tiles, T.uT_d, 512, evac)

    def phase_attn(l):
        new_phase()
        mixv = T.mixT_d.rearrange("(t p) (k s) -> p t k s", p=128, k=KC)
        q_sb = arena.alloc([S], BF16, "q_sb")
        k_sb = arena.alloc([S], BF16, "k_sb")
        NBmax = S // 128
        Vs = [arena.alloc([NBmax, 128], BF16, f"Vs{i}") for i in range(2)]
        Oacc = arena.alloc([S], F32, "Oacc")
        Dacc = arena.alloc([S], F32, "Dacc")
        mix_sb = arena.alloc([S], BF16, "mix_sb")
        Bt = [arena.alloc([256], F32, f"Bt{i}") for i in range(2)]
        tmpb = [arena.alloc([256], F32, f"tmp{i}") for i in range(4)]
        PT = [arena.alloc([256], BF16, f"PT{i}") for i in range(8)]
        esink = arena.alloc([1], F32, "esink")
        rden = [arena.alloc([128], F32, f"rden{i}") for i in range(2)]
        PS_S = [PS[0], PS[1], PS[2], PS[7]]
        PS_O = [PS[3], PS[4]]
        PS_D = [PS[5], PS[6]]
        inv_a = 1.0 / math.sqrt(128.0)
        inv_b = 1.0 / math.sqrt(64.0)
        cnt = {"s": 0, "o": 0, "bt": 0, "tmp": 0, "vs": 0}

        for h in range(HA):
            slope = float(cfg.slopes[HB + h])
            sch.op("sp", lambda E, h=h: E.dma_start(out=q_sb.ap, in_=T.qkT_d[h * 128:(h + 1) * 128, :]), writes=[q_sb], dma=True)
            sch.op("sp", lambda E, h=h: E.dma_start(out=k_sb.ap, in_=T.qkT_d[WA + h * 128:WA + (h + 1) * 128, :]), writes=[k_sb], dma=True)
            for bi, d in enumerate((1, 4, 16)):
                cval = np.float32(slope) * np.float32(d)
                B = Bt[cnt["bt"] % 2]; cnt["bt"] += 1
                sch.op("dve", lambda E, B=B, cval=cval: E.scalar_tensor_tensor(out=B.ap, in0=DD, scalar=-float(cval), in1=MA,
                                                                               op0=ALU.mult, op1=ALU.add),
                       reads=[CST], writes=[B])
                L = S // d
                nb = L // 128
                for rho in range(d):
                    V = Vs[cnt["vs"] % 2]; cnt["vs"] += 1
                    vsrc = T.v_d[:, h * 128:(h + 1) * 128]
                    for j0 in range(0, nb, 16):
                        j1 = min(nb, j0 + 16)
                        src = vsrc[rho + j0 * 128 * d: rho + (j1 * 128 - 1) * d + 1: d, :].rearrange("(j p) e -> p j e", p=128)
                        sch.op("sp", lambda E, V=V, j0=j0, j1=j1, src=src: E.dma_start(out=V.ap[:, j0:j1, :], in_=src),
                               writes=[V.sub(j0)], dma=True)
                    qv = q_sb.ap[:, rho::d] if d > 1 else q_sb.ap
                    kv = k_sb.ap[:, rho::d] if d > 1 else k_sb.ap
                    Ov = Oacc.ap[:, rho::d] if d > 1 else Oacc.ap
                    Dv = Dacc.ap[:, rho::d] if d > 1 else Dacc.ap
                    prevPT = None
                    for j in range(nb):
                        nq = 256 if j + 1 < nb else 128
                        P = PS_S[cnt["s"] % 4]
                        TM = tmpb[cnt["s"] % 4]
                        Pt = PT[cnt["s"] % 8]; cnt["s"] += 1
                        sch.op("pe", lambda E, P=P, kv=kv, qv=qv, j=j, nq=nq: E.matmul(
                            P.ap[:, 0:nq], lhsT=kv[:, j * 128:(j + 1) * 128], rhs=qv[:, j * 128:j * 128 + nq], start=True, stop=True),
                            reads=[k_sb, q_sb], writes=[P])
                        sch.op("dve", lambda E, P=P, TM=TM, B=B, nq=nq: E.scalar_tensor_tensor(
                            out=TM.ap[:, 0:nq], in0=P.ap[:, 0:nq], scalar=inv_a, in1=B.ap[:, 0:nq], op0=ALU.mult, op1=ALU.add),
                            reads=[P, B], writes=[TM])
                        sch.op("act", lambda E, TM=TM, Pt=Pt, nq=nq: E.activation(out=Pt.ap[:, 0:nq], in_=TM.ap[:, 0:nq], func=AF.Exp),
                               reads=[TM], writes=[Pt])
                        PO = PS_O[cnt["o"] % 2]; PD = PS_D[cnt["o"] % 2]; cnt["o"] += 1
                        jsub = (j // 16) * 16
                        if j > 0:
                            psub = ((j - 1) // 16) * 16
                            pp = prevPT
                            sch.op("pe", lambda E, PO=PO, V=V, pp=pp, j=j: E.matmul(PO.ap[:, 0:128], lhsT=V.ap[:, j - 1, :], rhs=pp.ap[:, 128:256],
                                                                                  start=True, stop=False),
                                   reads=[V.sub(psub), pp], writes=[PO])
                            sch.op("pe", lambda E, PD=PD, pp=pp: E.matmul(PD.ap[:, 0:128], lhsT=ones_b, rhs=pp.ap[:, 128:256], start=True, stop=False),
                                   reads=[CSTB, pp], writes=[PD])
                        sch.op("pe", lambda E, PO=PO, V=V, Pt=Pt, j=j: E.matmul(PO.ap[:, 0:128], lhsT=V.ap[:, j, :], rhs=Pt.ap[:, 0:128],
                                                                              start=(j == 0), stop=True),
                               reads=[V.sub(jsub), Pt], writes=[PO])
                        sch.op("pe", lambda E, PD=PD, Pt=Pt, j=j: E.matmul(PD.ap[:, 0:128], lhsT=ones_b, rhs=Pt.ap[:, 0:128], start=(j == 0), stop=True),
                               reads=[CSTB, Pt], writes=[PD])
                        osl = Ov[:, j * 128:(j + 1) * 128]
                        dsl = Dv[:, j * 128:(j + 1) * 128]
                        if bi == 0:
                            sch.op("act", lambda E, PO=PO, osl=osl: E.copy(out=osl, in_=PO.ap[:, 0:128]), reads=[PO], writes=[Oacc.sub(0)])
                            sch.op("dve", lambda E, PD=PD, dsl=dsl: E.tensor_copy(out=dsl, in_=PD.ap[:, 0:128]), reads=[PD], writes=[Dacc.sub(0)])
                        else:
                            sch.op("dve", lambda E, PO=PO, osl=osl: E.tensor_tensor(out=osl, in0=PO.ap[:, 0:128], in1=osl, op=ALU.add),
                                   reads=[PO, Oacc.sub(0)], writes=[Oacc.sub(0)])
                            sch.op("dve", lambda E, PD=PD, dsl=dsl: E.tensor_tensor(out=dsl, in0=PD.ap[:, 0:128], in1=dsl, op=ALU.add),
                                   reads=[PD, Dacc.sub(0)], writes=[Dacc.sub(0)])
                        prevPT = Pt
            for s0 in range(0, S, 2048):
                s1 = min(S, s0 + 2048)
                sch.op("dve", lambda E, s0=s0, s1=s1: E.reciprocal(out=Dacc.ap[:, s0:s1], in_=Dacc.ap[:, s0:s1]), reads=[Dacc], writes=[Dacc])
                sch.op("pool", lambda E, s0=s0, s1=s1: E.tensor_tensor(out=mix_sb.ap[:, s0:s1], in0=Oacc.ap[:, s0:s1], in1=Dacc.ap[:, s0:s1], op=ALU.mult),
                       reads=[Oacc, Dacc], writes=[mix_sb])
            sch.op("sp", lambda E, h=h: E.dma_start(out=mixv[:, :, h, :], in_=mix_sb.ap.rearrange("p (t s) -> p t s", s=512)), reads=[mix_sb], dma=True)

        Vz = [Vs[0], Vs[1]]
        nb = S // 128
        cur_kv = -1
        for pr in range(HB // 2):
            kvh = (2 * pr) // 8
            if kvh != cur_kv:
                cur_kv = kvh
                sch.op("pool", lambda E: E.memset(Vz[0].ap, 0.0), writes=[Vz[0]])
                sch.op("pool", lambda E: E.memset(Vz[1].ap, 0.0), writes=[Vz[1]])
                vsrc = T.v_d[:, WA + kvh * 64:WA + (kvh + 1) * 64]
                for j0 in range(0, nb, 16):
                    j1 = min(nb, j0 + 16)
                    src = vsrc[j0 * 128:j1 * 128, :].rearrange("(j p) e -> p j e", p=128)
                    sch.op("sp", lambda E, j0=j0, j1=j1, src=src: E.dma_start(out=Vz[0].ap[:, j0:j1, 0:64], in_=src), writes=[Vz[0]], dma=True)
                    sch.op("sp", lambda E, j0=j0, j1=j1, src=src: E.dma_start(out=Vz[1].ap[:, j0:j1, 64:128], in_=src), writes=[Vz[1]], dma=True)
                kr = 2 * WA + WB + kvh * 64
                sch.op("sp", lambda E, kr=kr: E.dma_start(out=k_sb.ap[0:64, :], in_=T.qkT_d[kr:kr + 64, :]), writes=[k_sb], dma=True)
                sch.op("sp", lambda E, kr=kr: E.dma_start(out=k_sb.ap[64:128, :], in_=T.qkT_d[kr:kr + 64, :]), writes=[k_sb], dma=True)
            qr = 2 * WA + pr * 128
            sch.op("sp", lambda E, qr=qr: E.dma_start(out=q_sb.ap, in_=T.qkT_d[qr:qr + 128, :]), writes=[q_sb], dma=True)
            srow = l * (HB // 2) + pr
            sch.op("sp", lambda E, srow=srow: E.dma_start(out=esink.ap, in_=T.sinks[srow:srow + 1, :].rearrange("o p -> p o")), writes=[esink], dma=True)
            sch.op("act", lambda E: E.activation(out=esink.ap, in_=esink.ap, func=AF.Exp), reads=[esink], writes=[esink])
            Bh = []
            for hh in range(2):
                slope = float(cfg.slopes[2 * pr + hh])
                B = Bt[hh]
                sch.op("dve", lambda E, B=B, slope=slope: E.scalar_tensor_tensor(out=B.ap, in0=DD, scalar=-slope, in1=MB, op0=ALU.mult, op1=ALU.add),
                       reads=[CST], writes=[B])
                Bh.append(B)
            prev = [None, None]
            for j in range(nb):
                nq = 256 if j + 1 < nb else 128
                cur = []
                for hh in range(2):
                    P = PS_S[cnt["s"] % 4]
                    TM = tmpb[cnt["s"] % 4]
                    Pt = PT[cnt["s"] % 8]; cnt["s"] += 1
                    lo, hi = hh * 64, (hh + 1) * 64
                    sch.op("pe", lambda E, P=P, j=j, nq=nq, lo=lo, hi=hi: E.matmul(
                        P.ap[:, 0:nq], lhsT=k_sb.ap[lo:hi, j * 128:(j + 1) * 128], rhs=q_sb.ap[lo:hi, j * 128:j * 128 + nq], start=True, stop=True),
                        reads=[k_sb, q_sb], writes=[P])
                    sch.op("dve", lambda E, P=P, TM=TM, hh=hh, nq=nq: E.scalar_tensor_tensor(
                        out=TM.ap[:, 0:nq], in0=P.ap[:, 0:nq], scalar=inv_b, in1=Bh[hh].ap[:, 0:nq], op0=ALU.mult, op1=ALU.add),
                        reads=[P, Bh[hh]], writes=[TM])
                    sch.op("act", lambda E, TM=TM, Pt=Pt, nq=nq: E.activation(out=Pt.ap[:, 0:nq], in_=TM.ap[:, 0:nq], func=AF.Exp),
                           reads=[TM], writes=[Pt])
                    cur.append(Pt)
                PO = PS_O[cnt["o"] % 2]; PD = PS_D[cnt["o"] % 2]
                R = rden[cnt["o"] % 2]; cnt["o"] += 1
                mms = []
                if j > 0:
                    for hh in range(2):
                        mms.append((Vz[hh], j - 1, prev[hh], 128, 256, hh))
                for hh in range(2):
                    mms.append((Vz[hh], j, cur[hh], 0, 128, hh))
                for mi, (Vb, jj, Pb, a0, a1, hh) in enumerate(mms):
                    first = (mi == 0); last = (mi == len(mms) - 1)
                    sch.op("pe", lambda E, PO=PO, Vb=Vb, jj=jj, Pb=Pb, a0=a0, a1=a1, first=first, last=last: E.matmul(
                        PO.ap[:, 0:128], lhsT=Vb.ap[:, jj, :], rhs=Pb.ap[:, a0:a1], start=first, stop=last),
                        reads=[Vb, Pb], writes=[PO])
                for mi, (Vb, jj, Pb, a0, a1, hh) in enumerate(mms):
                    first = (mi == 0); last = (mi == len(mms) - 1)
                    sch.op("pe", lambda E, PD=PD, Pb=Pb, a0=a0, a1=a1, hh=hh, first=first, last=last: E.matmul(
                        PD.ap[:, 0:128], lhsT=onesz[hh], rhs=Pb.ap[:, a0:a1], start=first, stop=last),
                        reads=[CSTB, Pb], writes=[PD])
                sch.op("dve", lambda E, PD=PD, R=R: E.tensor_scalar(out=R.ap, in0=PD.ap[:, 0:128], scalar1=esink.ap[:, 0:1], scalar2=None, op0=ALU.add),
                       reads=[PD, esink], writes=[R])
                sch.op("dve", lambda E, R=R: E.reciprocal(out=R.ap, in_=R.ap), reads=[R], writes=[R])
                sch.op("dve", lambda E, PO=PO, R=R, j=j: E.tensor_tensor(out=mix_sb.ap[:, j * 128:(j + 1) * 128], in0=PO.ap[:, 0:128], in1=R.ap, op=ALU.mult),
                       reads=[PO, R], writes=[mix_sb.sub(0)])
                prev = cur
            mk = WA // 128 + pr
            sch.op("sp", lambda E, mk=mk: E.dma_start(out=mixv[:, :, mk, :], in_=mix_sb.ap.rearrange("p (t s) -> p t s", s=512)), reads=[mix_sb], dma=True)

    def phase_projT(w_rows_ap, KCn, actT_d, TB):
        new_phase()
        ob = [arena.alloc([512], F32, f"po{i}") for i in range(4)]
        st = {"n": 0}

        def evac(ci, tt, P, c0, cw):
            O = ob[st["n"] % 4]; st["n"] += 1
            if st["n"] % 2 == 0:
                sch.op("act", lambda E: E.copy(out=O.ap[:, 0:cw], in_=P.ap[:, 0:cw]), reads=[P], writes=[O])
            else:
                sch.op("dve", lambda E: E.tensor_copy(out=O.ap[:, 0:cw], in_=P.ap[:, 0:cw]), reads=[P], writes=[O])
            sch.op("sp", lambda E: E.dma_start(out=T.y_d[tt * 128:(tt + 1) * 128, c0:c0 + cw], in_=O.ap[:, 0:cw]), reads=[O], dma=True)

        gemm_T(w_rows_ap, KCn, [(c0, 512) for c0 in range(0, D, 512)], actT_d, TB, evac)

    def phase_ln(l, xres_d, m_gate, g_in, b_in, dst_d):
        new_phase()
        gate = arena.alloc([D], F32, "gate")
        gbc = arena.alloc([D], F32, "gbc")
        bbc = arena.alloc([D], F32, "bbc")
        sch.op("sp", lambda E: E.dma_start(out=gate.ap, in_=T.mod_d[l:l + 1, m_gate * D:(m_gate + 1) * D].partition_broadcast(128)), writes=[gate], dma=True)
        sch.op("sp", lambda E: E.dma_start(out=gbc.ap, in_=g_in[l:l + 1, :].partition_broadcast(128)), writes=[gbc], dma=True)
        sch.op("sp", lambda E: E.dma_start(out=bbc.ap, in_=b_in[l:l + 1, :].partition_broadcast(128)), writes=[bbc], dma=True)
        sch.op("pool", lambda E: E.tensor_scalar(out=gate.ap, in0=gate.ap, scalar1=1.0, scalar2=None, op0=ALU.add), reads=[gate], writes=[gate])
        xb = [arena.alloc([D], F32, f"lx{i}") for i in range(2)]
        yb = [arena.alloc([D], F32, f"ly{i}") for i in range(2)]
        zb = [arena.alloc([D], F32, f"lz{i}") for i in range(2)]
        stat = [arena.alloc([8], F32, f"st{i}") for i in range(2)]
        lnload = set()

        def load_xy(t_):
            if t_ < NT and t_ not in lnload:
                X_ = xb[t_ % 2]; Y_ = yb[t_ % 2]
                sch.op("sp", lambda E, X_=X_, t_=t_: E.dma_start(out=X_.ap, in_=xres_d[t_ * 128:(t_ + 1) * 128, :]), writes=[X_], dma=True)
                sch.op("sp", lambda E, Y_=Y_, t_=t_: E.dma_start(out=Y_.ap, in_=T.y_d[t_ * 128:(t_ + 1) * 128, :]), writes=[Y_], dma=True)
                lnload.add(t_)

        for tt in range(NT):
            X = xb[tt % 2]; Y = yb[tt % 2]; Z = zb[tt % 2]; St = stat[tt % 2]
            load_xy(tt)
            sch.op("pool", lambda E, Y=Y: E.tensor_tensor(out=Y.ap, in0=Y.ap, in1=gate.ap, op=ALU.mult), reads=[Y, gate], writes=[Y])
            sch.op("dve", lambda E, X=X, Y=Y, Z=Z, St=St: E.scalar_tensor_tensor(out=Z.ap, in0=X.ap, scalar=float(cfg.alpha), in1=Y.ap,
                                                                                 op0=ALU.mult, op1=ALU.add, accum_out=St.ap[:, 0:1]),
                   reads=[X, Y], writes=[Z, St])
            sch.op("dve", lambda E, St=St: E.tensor_scalar(out=St.ap[:, 1:2], in0=St.ap[:, 0:1], scalar1=1.0 / D, scalar2=None, op0=ALU.mult),
                   reads=[St], writes=[St])
            sch.op("dve", lambda E, Z=Z, St=St: E.tensor_scalar(out=Z.ap, in0=Z.ap, scalar1=St.ap[:, 1:2], scalar2=None, op0=ALU.subtract),
                   reads=[Z, St], writes=[Z])
            sch.op("act", lambda E, Z=Z, X=X, St=St: E.activation(out=X.ap, in_=Z.ap, func=AF.Square, accum_out=St.ap[:, 2:3]),
                   reads=[Z], writes=[X, St])
            sch.op("dve", lambda E, St=St: E.tensor_scalar(out=St.ap[:, 3:4], in0=St.ap[:, 2:3], scalar1=1.0 / D, scalar2=LN_EPS, op0=ALU.mult, op1=ALU.add),
                   reads=[St], writes=[St])
            sch.op("act", lambda E, St=St: E.activation(out=St.ap[:, 4:5], in_=St.ap[:, 3:4], func=AF.Sqrt), reads=[St], writes=[St])
            sch.op("dve", lambda E, St=St: E.reciprocal(out=St.ap[:, 5:6], in_=St.ap[:, 4:5]), reads=[St], writes=[St])
            sch.op("dve", lambda E, Z=Z, St=St: E.scalar_tensor_tensor(out=Z.ap, in0=Z.ap, scalar=St.ap[:, 5:6], in1=gbc.ap, op0=ALU.mult, op1=ALU.mult),
                   reads=[Z, St, gbc], writes=[Z])
            sch.op("pool", lambda E, Z=Z: E.tensor_tensor(out=Z.ap, in0=Z.ap, in1=bbc.ap, op=ALU.add), reads=[Z, bbc], writes=[Z])
            load_xy(tt + 1)
            sch.op("sp", lambda E, Z=Z, tt=tt: E.dma_start(out=dst_d[tt * 128:(tt + 1) * 128, :], in_=Z.ap), reads=[Z], dma=True)

    def phase_up(l):
        new_phase()
        wl = T.w_up[l * D:(l + 1) * D, :]
        CP = 2
        assert FC % CP == 0
        cwt = arena.alloc([3, 128], F32, "cwt")
        cwp = [arena.alloc([4], F32, f"cwp{i}") for i in range(2 * CP * 2)]
        carry = [arena.alloc([2], F32, f"carry{i}") for i in range(2 * CP)]
        acc = [arena.alloc([512], F32, f"acc{i}") for i in range(2 * CP * 2)]
        aout = [arena.alloc([CP, 512], BF16, f"aout{i}") for i in range(2)]
        rowt = [arena.alloc([128], F32, f"rowt{i}") for i in range(2)]
        wv = wl.rearrange("(k p) c -> p k c", p=128)
        aTv = T.aT_d.rearrange("(t p) (k s) -> p t k s", p=128, k=FC)
        wbuf = [arena.alloc([KC, 2 * CP * 128], BF16, f"uw{i}") for i in range(2)]
        abuf = [arena.alloc([KC, 512], BF16, f"ua{i}") for i in range(2)]
        ai = 0
        pi = 0
        ao = 0
        loaded = {}
        n_it = (FC // CP) * NB512

        def load_a(it):
            if it < n_it and it not in loaded:
                A_ = abuf[it % 2]
                tb_ = it % NB512
                sch.op("sp", lambda E, A_=A_, tb_=tb_: E.dma_start(out=A_.ap, in_=blk(T.uT_d, KC, tb_)), writes=[A_], dma=True)
                loaded[it] = A_

        for gi in range(FC // CP):
            W = wbuf[gi % 2]
            cols = []
            for i in range(CP):
                cols.append((gi * CP + i) * 128)
            for i in range(CP):
                cols.append(DFF + (gi * CP + i) * 128)
            for half in range(2):
                c0h = cols[half * CP]
                sch.op("pool", lambda E, W=W, half=half, c0h=c0h: E.dma_start(out=W.ap[:, :, half * CP * 128:(half + 1) * CP * 128],
                                                                           in_=wv[:, :, c0h:c0h + CP * 128]),
                       writes=[W.sub(half * CP + i_) for i_ in range(CP)], dma=True)
            for ti, c0 in enumerate(cols):
                RT = rowt[(gi * 2 * CP + ti) % 2]
                CW = cwp[(gi % 2) * 2 * CP + ti]
                sch.op("sp", lambda E, RT=RT, c0=c0: E.dma_start(out=RT.ap[0:3, :], in_=T.conv_w[l * 3:(l + 1) * 3, c0:c0 + 128]), writes=[RT], dma=True)
                sch.op("sp", lambda E, RT=RT, c0=c0: E.dma_start(out=RT.ap[3:4, :], in_=T.conv_b[l:l + 1, c0:c0 + 128]), writes=[RT], dma=True)
                PX = PS[pi % 8]; pi += 1
                sch.op("pe", lambda E, PX=PX, RT=RT: E.transpose(out=PX.ap[:, 0:4], in_=RT.ap[0:4, :], identity=ident[0:4, 0:4]),
                       reads=[RT, CST], writes=[PX])
                sch.op("dve", lambda E, PX=PX, CW=CW: E.tensor_copy(out=CW.ap, in_=PX.ap[:, 0:4]), reads=[PX], writes=[CW])
                sch.op("pool", lambda E, ti=ti: E.memset(carry[ti].ap, 0.0), writes=[carry[ti]])
            for tb in range(NB512):
                load_a(ai)
                A = loaded[ai]
                load_a(ai + 1)
                ai += 1
                accs = []
                for ti in range(2 * CP):
                    P = PS[pi % 8]; pi += 1
                    for k in range(KC):
                        sch.op("pe", lambda E, P=P, W=W, A=A, k=k, ti=ti: E.matmul(
                            P.ap[:, :], lhsT=W.ap[:, k, ti * 128:(ti + 1) * 128], rhs=A.ap[:, k, :], start=(k == 0), stop=(k == KC - 1)),
                            reads=[W.sub(ti), A], writes=[P])
                    CW = cwp[(gi % 2) * 2 * CP + ti]
                    AC = acc[(tb % 2) * 2 * CP + ti]
                    CA = carry[ti]
                    sch.op("act", lambda E, P=P, AC=AC, CW=CW: E.activation(out=AC.ap, in_=P.ap, func=AF.Identity, scale=CW.ap[:, 2:3], bias=CW.ap[:, 3:4]),
                           reads=[P, CW], writes=[AC])
                    sch.op("dve", lambda E, P=P, AC=AC, CW=CW: E.scalar_tensor_tensor(out=AC.ap[:, 1:512], in0=P.ap[:, 0:511], scalar=CW.ap[:, 1:2],
                                                                                      in1=AC.ap[:, 1:512], op0=ALU.mult, op1=ALU.add),
                           reads=[P, CW, AC], writes=[AC])
                    sch.op("dve", lambda E, P=P, AC=AC, CW=CW: E.scalar_tensor_tensor(out=AC.ap[:, 2:512], in0=P.ap[:, 0:510], scalar=CW.ap[:, 0:1],
                                                                                      in1=AC.ap[:, 2:512], op0=ALU.mult, op1=ALU.add),
                           reads=[P, CW, AC], writes=[AC])
                    sch.op("dve", lambda E, AC=AC, CW=CW, CA=CA: E.scalar_tensor_tensor(out=AC.ap[:, 0:1], in0=CA.ap[:, 1:2], scalar=CW.ap[:, 1:2],
                                                                                        in1=AC.ap[:, 0:1], op0=ALU.mult, op1=ALU.add),
                           reads=[CA, CW, AC], writes=[AC])
                    sch.op("dve", lambda E, AC=AC, CW=CW, CA=CA: E.scalar_tensor_tensor(out=AC.ap[:, 0:2], in0=CA.ap[:, 0:2], scalar=CW.ap[:, 0:1],
                                                                                        in1=AC.ap[:, 0:2], op0=ALU.mult, op1=ALU.add),
                           reads=[CA, CW, AC], writes=[AC])
                    sch.op("dve", lambda E, P=P, CA=CA: E.tensor_copy(out=CA.ap, in_=P.ap[:, 510:512]), reads=[P], writes=[CA])
                    accs.append(AC)
                AO = aout[ao % 2]; ao += 1
                for i in range(CP):
                    G_ = accs[i]; V_ = accs[CP + i]
                    sch.op("act", lambda E, G_=G_: E.activation(out=G_.ap, in_=G_.ap, func=AF.Silu), reads=[G_], writes=[G_])
                    sch.op("pool", lambda E, G_=G_, V_=V_, AO=AO, i=i: E.tensor_tensor(out=AO.ap[:, i, :], in0=G_.ap, in1=V_.ap, op=ALU.mult),
                           reads=[G_, V_], writes=[AO.sub(i)])
                r0 = gi * CP * 128
                k0 = gi * CP
                for i in range(CP):
                    dst = aTv[:, tb * 4:(tb + 1) * 4, k0 + i, :]
                    sch.op("sp", lambda E, AO=AO, dst=dst, i=i: E.dma_start(out=dst, in_=AO.ap[:, i, :].rearrange("p (j s) -> p j s", j=4)),
                           reads=[AO.sub(i)], dma=True)

    for (ph, l) in plan["phases"]:
        if ph == "mod":
            phase_mod()
        elif ph == "A":
            cur = T.x if (l == 0 or plan.get("layer_local")) else T.x2_d
            phase_u(l, cur, 0, 1)
            phase_inproj(l)
            phase_vproj(l)
            phase_attn(l)
            phase_projT(T.w_out[l * D:(l + 1) * D, :], KC, T.mixT_d, 512)
            phase_ln(l, cur, 2, T.ln1_g, T.ln1_b, T.x1_d)
        elif ph == "B":
            phase_u(l, T.x1_d, 3, 4)
            phase_up(l)
            phase_projT(T.w_down[l * DFF:(l + 1) * DFF, :], FC, T.aT_d, 128)
            phase_ln(l, T.x1_d, 5, T.ln2_g, T.ln2_b, T.x2_d)
    sch.fence()
    with nc.Block() as block:
        sch.emit(block)
    return nc, used_inputs


def make_consts():
    k = np.arange(128, dtype=np.float32)[:, None]
    q = np.arange(128, dtype=np.float32)[None, :]
    ddiag = q - k
    dprev = 128.0 + q - k
    DD = np.concatenate([ddiag, dprev], axis=1)
    madiag = np.where(ddiag >= 0, 0.0, NEG)
    MA = np.concatenate([madiag, np.where(dprev <= 128, 0.0, NEG)], axis=1)
    MB = np.concatenate([madiag, np.where(dprev <= 127, 0.0, NEG)], axis=1)
    DDm = DD.copy()
    ident = np.eye(128, dtype=np.float32)
    return np.ascontiguousarray(np.concatenate([ident, DDm, MA, MB], axis=1).astype(np.float32))


def _f(a):
    return np.ascontiguousarray(np.asarray(a, dtype=np.float32))


def host_arrays(cfg, inputs):
    S, D, DEPTH = cfg.S, cfg.D, cfg.DEPTH
    sk = _f(inputs["sinks"]).reshape(DEPTH, cfg.HB // 2, 2)
    sk = np.ascontiguousarray(np.repeat(sk, 64, axis=2))
    return {
        "x": _f(inputs["x"]).reshape(S, D),
        "c": _f(inputs["c"]).reshape(cfg.KC, 128),
        "w_mod": _f(inputs["w_mod"]), "b_mod": _f(inputs["b_mod"]),
        "w_in": _f(inputs["w_in"]), "sinks": sk, "w_out": _f(inputs["w_out"]),
        "ln1_g": _f(inputs["ln1_g"]), "ln1_b": _f(inputs["ln1_b"]),
        "w_up": _f(inputs["w_up"]), "conv_w": _f(inputs["conv_w"]), "conv_b": _f(inputs["conv_b"]),
        "w_down": _f(inputs["w_down"]), "ln2_g": _f(inputs["ln2_g"]), "ln2_b": _f(inputs["ln2_b"]),
    }


def layer_slice(name, arr, l0, l1):
    a = arr[l0:l1]
    if a.ndim == 3:
        return np.ascontiguousarray(a.reshape(a.shape[0] * a.shape[1], a.shape[2]))
    return np.ascontiguousarray(a)


_PROG_CACHE = {}


def launch(cfg, plan, host, l0, l1, extra, debug=False, trace=False):
    key = (cfg.S, cfg.D, tuple(plan["phases"]), plan.get("depth", 1), debug)
    if key not in _PROG_CACHE:
        _PROG_CACHE[key] = build_program(cfg, plan, debug=debug)
    nc, used = _PROG_CACHE[key]
    im = {}
    for n in used:
        if n in extra:
            im[n] = extra[n]
        elif n == "consts":
            im[n] = make_consts()
        elif n in ("x", "c"):
            im[n] = host[n]
        else:
            im[n] = layer_slice(n, host[n], l0, l1)
    res = run_bass_kernel_spmd(nc, [im], core_ids=[0], trace=trace)
    return res.results[0]


def run_fused(cfg, inputs, debug=False):
    host = host_arrays(cfg, inputs)
    phases = [("mod", 0)]
    for l in range(cfg.DEPTH):
        phases += [("A", l), ("B", l)]
    plan = dict(phases=phases, depth=cfg.DEPTH, ext_out={"x2_d"})
    r = launch(cfg, plan, host, 0, cfg.DEPTH, {}, debug=debug)
    return r


def run_multi(cfg, inputs):
    host = host_arrays(cfg, inputs)
    r = launch(cfg, dict(phases=[("mod", 0)], depth=cfg.DEPTH, ext_out={"mod_d"}), host, 0, cfg.DEPTH, {})
    mod = np.asarray(r["mod_d"], dtype=np.float32)
    x = host["x"]
    planA = dict(phases=[("A", 0)], depth=1, layer_local=True, ext_in={"mod_d"}, ext_out={"x1_d"})
    planB = dict(phases=[("B", 0)], depth=1, layer_local=True, ext_in={"mod_d", "x1_d"}, ext_out={"x2_d"})
    for l in range(cfg.DEPTH):
        ml = np.ascontiguousarray(mod[l:l + 1])
        r = launch(cfg, planA, host, l, l + 1, {"mod_d": ml, "x": x})
        x1 = np.ascontiguousarray(np.asarray(r["x1_d"], dtype=np.float32))
        r = launch(cfg, planB, host, l, l + 1, {"mod_d": ml, "x1_d": x1})
        x = np.ascontiguousarray(np.asarray(r["x2_d"], dtype=np.float32))
    return x


def kernel(**inputs):
    cfg = Cfg()
    o = run_multi(cfg, inputs)
    return np.asarray(o, dtype=np.float32).reshape(1, cfg.S, cfg.D)
```
